# Optimizing a Trainium2 kernel written in Bass

```python
import jax, jax.numpy as jnp
from jax import lax

D_MODEL = 1024
BATCH = 4
SEQ = 4096
DEPTH = 1

N_ATTN_HEADS = 8
HEAD_DIM = 64
ATTN_WIDTH = N_ATTN_HEADS * HEAD_DIM
N_CONV_GROUPS = 8
CONV_GROUP_DIM = 64
CONV_WIDTH = N_CONV_GROUPS * CONV_GROUP_DIM
CONV_KSIZE = 3
D_FF = 2816
Q_BLOCK = 128
N_SUBLAYERS = 3
N_MOD = 3
EPS = 1e-6
FFN_RES_WEIGHT = 0.5
MIX_IN_WIDTH = 3 * CONV_WIDTH + 3 * ATTN_WIDTH + 2 * D_MODEL

kernel_name = "hybrid_shortconv_stickbreaking_macaron_block"


def rmsnorm(x, g):
    xf = x.astype(jnp.float32)
    inv = lax.rsqrt(jnp.mean(xf * xf, axis=-1, keepdims=True) + EPS)
    return (xf * inv).astype(x.dtype) * g


def modulate(x, shift, scale):
    return x * (1 + scale[:, None, :]) + shift[:, None, :]


def swiglu(x, w_gu, w_down):
    g, u = jnp.split(x @ w_gu, 2, axis=-1)
    return (jax.nn.silu(g) * u) @ w_down


def short_conv(b_gate, c_gate, xin, conv_w):
    v = c_gate * xin
    S = v.shape[1]
    vp = jnp.pad(v, ((0, 0), (CONV_KSIZE - 1, 0), (0, 0)))
    y = conv_w[0] * vp[:, 0:S, :]
    for k in range(1, CONV_KSIZE):
        y = y + conv_w[k] * vp[:, k:k + S, :]
    return b_gate * y


def stick_breaking_attention(q, k, v):
    S = q.shape[2]
    scale = HEAD_DIM ** -0.5
    qf = q.astype(jnp.float32)
    kf = k.astype(jnp.float32)
    vf = v.astype(jnp.float32)
    outs = []
    for i in range(S // Q_BLOCK):
        start = i * Q_BLOCK
        end = start + Q_BLOCK
        q_blk = qf[:, :, start:end, :]
        k_c = kf[:, :, :end, :]
        v_c = vf[:, :, :end, :]
        z = jnp.einsum('bhqd,bhkd->bhqk', q_blk, k_c) * scale
        t_pos = start + jnp.arange(Q_BLOCK)
        s_pos = jnp.arange(end)
        valid = s_pos[None, :] < t_pos[:, None]
        sp = jnp.where(valid, jax.nn.softplus(z), 0.0)
        rem = lax.cumsum(sp, axis=3, reverse=True) - sp
        log_a = jax.nn.log_sigmoid(z) - rem
        a = jnp.where(valid, jnp.exp(log_a), 0.0)
        outs.append(jnp.einsum('bhqk,bhkd->bhqd', a, v_c))
    return jnp.concatenate(outs, axis=2).astype(q.dtype)


def mixer(u, w_mix_in, b_merge, conv_w, w_conv_out, w_attn_out, w_out):
    B, S, _ = u.shape
    proj = u @ w_mix_in
    idx = [CONV_WIDTH, 2 * CONV_WIDTH, 3 * CONV_WIDTH,
           3 * CONV_WIDTH + ATTN_WIDTH, 3 * CONV_WIDTH + 2 * ATTN_WIDTH,
           3 * CONV_WIDTH + 3 * ATTN_WIDTH, 3 * CONV_WIDTH + 3 * ATTN_WIDTH + D_MODEL]
    cb, cc, cx, q, k, v, ga, gb = jnp.split(proj, idx, axis=-1)
    ya = short_conv(cb, cc, cx, conv_w) @ w_conv_out
    def heads(t):
        return t.reshape(B, S, N_ATTN_HEADS, HEAD_DIM).transpose(0, 2, 1, 3)
    o = stick_breaking_attention(heads(q), heads(k), heads(v))
    o = o.transpose(0, 2, 1, 3).reshape(B, S, ATTN_WIDTH)
    yb = o @ w_attn_out
    merged = jax.nn.sigmoid(ga + b_merge[0]) * ya + jax.nn.sigmoid(gb + b_merge[1]) * yb
    return merged @ w_out


def setup_inputs(seed: int = 0) -> dict:
    key = jax.random.key(seed)
    ks = jax.random.split(key, 24)
    f32 = jnp.float32
    L, D = DEPTH, D_MODEL

    def nrm(k, shape, s):
        return jax.random.normal(k, shape, f32) * s

    return {
        "x": nrm(ks[0], (BATCH, SEQ, D), 1.0),
        "c": nrm(ks[1], (BATCH, D), 1.0),
        "w_ada": nrm(ks[2], (L, D, N_SUBLAYERS * N_MOD * D), 0.5 * D ** -0.5),
        "b_ada": nrm(ks[3], (L, N_SUBLAYERS * N_MOD * D), 0.02),
        "norm1_g": 1.0 + nrm(ks[4], (L, D), 0.02),
        "ffn1_w_gu": nrm(ks[5], (L, D, 2 * D_FF), D ** -0.5),
        "ffn1_w_down": nrm(ks[6], (L, D_FF, D), D_FF ** -0.5),
        "norm2_g": 1.0 + nrm(ks[7], (L, D), 0.02),
        "w_mix_in": nrm(ks[8], (L, D, MIX_IN_WIDTH), D ** -0.5),
        "b_merge": nrm(ks[9], (L, 2, D), 0.02),
        "conv_w": nrm(ks[10], (L, CONV_KSIZE, CONV_WIDTH), CONV_KSIZE ** -0.5),
        "w_conv_out": nrm(ks[11], (L, CONV_WIDTH, D), CONV_WIDTH ** -0.5),
        "w_attn_out": nrm(ks[12], (L, ATTN_WIDTH, D), ATTN_WIDTH ** -0.5),
        "w_out": nrm(ks[13], (L, D, D), D ** -0.5),
        "norm3_g": 1.0 + nrm(ks[14], (L, D), 0.02),
        "ffn2_w_gu": nrm(ks[15], (L, D, 2 * D_FF), D ** -0.5),
        "ffn2_w_down": nrm(ks[16], (L, D_FF, D), D_FF ** -0.5),
        "final_g": 1.0 + nrm(ks[17], (D,), 0.02),
    }


def reference(x, c, w_ada, b_ada, norm1_g, ffn1_w_gu, ffn1_w_down, norm2_g,
              w_mix_in, b_merge, conv_w, w_conv_out, w_attn_out, w_out,
              norm3_g, ffn2_w_gu, ffn2_w_down, final_g):
    B = x.shape[0]
    c_act = jax.nn.silu(c)
    h = x
    for l in range(DEPTH):
        mod = (c_act @ w_ada[l] + b_ada[l]).reshape(B, N_SUBLAYERS, N_MOD, D_MODEL)
        u = modulate(rmsnorm(h, norm1_g[l]), mod[:, 0, 0], mod[:, 0, 1])
        h = h + FFN_RES_WEIGHT * mod[:, 0, 2][:, None, :] * swiglu(u, ffn1_w_gu[l], ffn1_w_down[l])
        u = modulate(rmsnorm(h, norm2_g[l]), mod[:, 1, 0], mod[:, 1, 1])
        y = mixer(u, w_mix_in[l], b_merge[l], conv_w[l], w_conv_out[l], w_attn_out[l], w_out[l])
        h = h + mod[:, 1, 2][:, None, :] * y
        u = modulate(rmsnorm(h, norm3_g[l]), mod[:, 2, 0], mod[:, 2, 1])
        h = h + FFN_RES_WEIGHT * mod[:, 2, 2][:, None, :] * swiglu(u, ffn2_w_gu[l], ffn2_w_down[l])
    return rmsnorm(h, final_g)
```

```python
import numpy as np
import concourse.bass as bass
import concourse.mybir as mybir
from concourse.bass_utils import run_bass_kernel_spmd

F32 = mybir.dt.float32
BF16 = mybir.dt.bfloat16
U8 = mybir.dt.uint8
AF = mybir.ActivationFunctionType
ALU = mybir.AluOpType

D = 1024
KC = 8
DFF = 2816
NJ = 22
T = 2048
NBLK = 16
EPS = 1e-6
SAME_ENG_SYNC = True
DEBUG_TAP = None


class Prog:
    def __init__(self):
        self.ops = []
        self.lastw = {}
        self.readers = {}
        self.dma_cnt = {}
        self.pending_bar = {}
        self.last_on_eng = {}
        self.last_dma = {}

    def add(self, eng, fn, r=(), w=(), dma=None):
        i = len(self.ops)
        deps = set()
        for k in r:
            if k in self.lastw:
                deps.add(self.lastw[k])
        for k in w:
            if k in self.lastw:
                deps.add(self.lastw[k])
            deps |= self.readers.get(k, set())
        if eng in self.pending_bar:
            deps |= self.pending_bar.pop(eng)
        deps.discard(i)
        op = dict(id=i, eng=eng, fn=fn, deps=deps, dma=dma, sig=False, cnt=0, seq=0)
        if dma is not None:
            self.dma_cnt[dma] = self.dma_cnt.get(dma, 0) + 1
            op["cnt"] = self.dma_cnt[dma]
            self.last_dma[dma] = i
        for k in r:
            self.readers.setdefault(k, set()).add(i)
        for k in w:
            self.lastw[k] = i
            self.readers[k] = set()
        self.ops.append(op)
        self.last_on_eng[eng] = i
        return i

    def barrier(self):
        allp = set(self.last_on_eng.values()) | set(self.last_dma.values())
        for e in ("pe", "act", "dve", "pool", "sp"):
            self.pending_bar[e] = set(allp) | self.pending_bar.get(e, set())

    def emit(self, nc, block, sems):
        ops = self.ops
        for op in ops:
            keep = set()
            for d in op["deps"]:
                y = ops[d]
                if y["dma"] is None and op["dma"] is None and y["eng"] == op["eng"]:
                    if op["eng"] == "pe" or not SAME_ENG_SYNC:
                        continue
                if y["dma"] is None and y["eng"] == op["eng"] and op["dma"] is not None and False:
                    continue
                keep.add(d)
            best = {}
            for d in keep:
                y = ops[d]
                key = ("d", y["dma"]) if y["dma"] is not None else ("e", y["eng"])
                if key not in best or best[key] < d:
                    best[key] = d
            keep = set(best.values())
            op["deps"] = keep
            for d in keep:
                if ops[d]["dma"] is None:
                    ops[d]["sig"] = True
        seqc = {}
        for op in ops:
            if op["dma"] is None and op["sig"]:
                seqc[op["eng"]] = seqc.get(op["eng"], 0) + 1
                op["seq"] = seqc[op["eng"]]
        dma_keys = sorted(self.dma_cnt.keys(), key=str)
        self.stats = dict(n_ops=len(ops), seq=dict(seqc), n_dma_keys=len(dma_keys))
        eng_sem = {e: sems[n] for n, e in enumerate(("pe", "act", "dve", "pool", "sp"))}
        dma_sem = {k: sems[5 + n] for n, k in enumerate(dma_keys)}
        assert 5 + len(dma_keys) <= len(sems), (len(dma_keys), len(sems))

        def run(engname, eng):
            waited = {}
            for op in ops:
                if op["eng"] != engname:
                    continue
                for d in sorted(op["deps"]):
                    y = ops[d]
                    if y["dma"] is not None:
                        s, v, key = dma_sem[y["dma"]], 16 * y["cnt"], ("d", y["dma"])
                    else:
                        s, v, key = eng_sem[y["eng"]], y["seq"], ("e", y["eng"])
                    if waited.get(key, 0) < v:
                        eng.wait_ge(s, v)
                        waited[key] = v
                ins = op["fn"](eng)
                if op["dma"] is not None:
                    ins.then_inc(dma_sem[op["dma"]], 16)
                elif op["sig"]:
                    ins.then_inc(eng_sem[op["eng"]], 1)
            if engname == "sp":
                for k in dma_keys:
                    v = 16 * self.dma_cnt[k]
                    if waited.get(("d", k), 0) < v:
                        eng.wait_ge(dma_sem[k], v)

        @block.tensor
        def _(e):
            run("pe", e)

        @block.scalar
        def _(e):
            run("act", e)

        @block.vector
        def _(e):
            run("dve", e)

        @block.gpsimd
        def _(e):
            run("pool", e)

        @block.sync
        def _(e):
            run("sp", e)


def build_program():
    nc = bass.Bass("TRN2", target_bir_lowering=False)

    def din(name, shape, dt=F32):
        return nc.dram_tensor(name, list(shape), dt, kind="ExternalInput").ap()

    xo_d = din("xo", [T, D])
    xp_d = din("xp", [T, D])
    crep_d = din("crep", [128, D])
    wada_d = din("wada", [72, 128, 1024])
    wgu_d = [din("wgu1", [11, 128, 4096]), din("wgu2", [11, 128, 4096])]
    wd_d = [din("wd1", [8, 128, DFF]), din("wd2", [8, 128, DFF])]
    wmix_d = din("wmix", [10, 128, 4096])
    wco_d = din("wco", [128, 4096])
    wao_d = din("wao", [128, 4096])
    wo_d = din("wo", [2, 128, 4096])
    vecs_d = din("vecs", [128, 48])
    bada_d = din("bada", [128, 72])
    convw_d = din("convw", [128, 12])
    ident_d = din("ident", [128, 128])
    tri_d = din("tri", [128, 3, 128])
    mk_d = din("mk", [128, 2, 128])
    om_d = din("om", [128, 1])
    y_d = nc.dram_tensor("y", [T, D], F32, kind="ExternalOutput").ap()
    dbg_d = None
    if DEBUG_TAP is not None:
        dbg_d = nc.dram_tensor("dbg", [128, KC, T], F32, kind="ExternalOutput").ap()
    kd_d = nc.dram_tensor("kd", [4, 128, 32, 128], BF16, kind="Internal").ap()
    qd_d = nc.dram_tensor("qd", [4, 128, T], BF16, kind="Internal").ap()
    vd_d = nc.dram_tensor("vd", [32, 128, 512], BF16, kind="Internal").ap()

    import contextlib
    es = contextlib.ExitStack()

    def sb(name, shape, dt):
        return es.enter_context(nc.sbuf_tensor(name, list(shape), dt))

    with es:
        hT = sb("hT", [128, KC, T], F32)
        wbf = [sb(f"wbf{i}", [128, 4096], BF16) for i in range(3)]
        identF = sb("identF", [128, 128], F32)
        tri = sb("trib", [128, 3, 128], BF16)
        mk = sb("mkb", [128, 2, 128], BF16)
        om = sb("oms", [128, 1], F32)
        vecs = sb("vecss", [128, 48], F32)
        bada = sb("badas", [128, 72], F32)
        convw = sb("convws", [128, 12], F32)
        modT = sb("modT", [128, 72], F32)
        AB = sb("AB", [128, 9, KC], F32)
        rstd2 = sb("rstd2", [128, T], F32)
        halo = sb("halo", [128, 4, NBLK, 2], F32)
        fgv = sb("fgv", [128, KC], F32)
        R1 = sb("R1", [128, 108544], U8)
        psp = [es.enter_context(nc.psum_tensor(f"ps{i}", [128, 1024], F32)) for i in range(4)]
        psb = [psp[i // 2][:, (i % 2) * 512:(i % 2 + 1) * 512] for i in range(8)]
        sems = [es.enter_context(nc.semaphore(f"s{i}")) for i in range(60)]
        block = es.enter_context(nc.Block())

        P = Prog()
        off = [0]

        def carve(nbytes, dt, shape=None):
            a = R1[:, off[0]:off[0] + nbytes].bitcast(dt)
            off[0] += nbytes
            assert off[0] <= 108544, off[0]
            return a

        def r3(ap, a):
            return ap.rearrange("p (a b) -> p a b", a=a)

        off[0] = 0
        UT = r3(carve(16384, BF16), KC)
        ACTB = r3(carve(45056, BF16), NJ)
        XS = [carve(4096, F32) for _ in range(2)]
        SQ = r3(carve(8192, BF16), KC)
        LNV = carve(2048, F32)
        RSTD = carve(2048, F32)
        TMP = [carve(2048, F32) for _ in range(2)]
        SG = [carve(2048, F32) for _ in range(2)]
        QS = [carve(1024, BF16) for _ in range(2)]
        UT2 = r3(carve(16384, BF16), KC)
        assert off[0] <= 108544
        WAS = [R1[:, 16384 + i * 16384:16384 + (i + 1) * 16384].bitcast(F32) for i in range(2)]

        psrot = [0]

        def ps_next():
            b = psrot[0] % 8
            psrot[0] += 1
            return b

        wrot = [0]

        def wload(src_ap, n):
            s = wrot[0] % 3
            wrot[0] += 1
            dst = wbf[s][:, 0:n]
            P.add("pool", lambda e, dst=dst, src_ap=src_ap: e.dma_start(out=dst, in_=src_ap, max_dma_last_dim=4096),
                  w=[("wbf", s)], dma=("wbf", s))
            return s

        def cdma(dst, src, key, eng="sp"):
            P.add(eng, lambda e, dst=dst, src=src: e.dma_start(out=dst, in_=src), w=[key], dma="const")

        cdma(identF[:], ident_d, "c_ident")
        cdma(vecs[:], vecs_d, "c_vecs")
        cdma(bada[:], bada_d, "c_bada")
        cdma(convw[:], convw_d, "c_convw")
        cdma(om[:], om_d, "c_om")
        P.add("pool", lambda e: e.dma_start(out=tri[:], in_=tri_d), w=["c_tri"], dma="constp")
        P.add("pool", lambda e: e.dma_start(out=mk[:], in_=mk_d), w=["c_mk"], dma="constp")
        CONST = ["c_ident", "c_vecs", "c_bada", "c_convw", "c_tri", "c_mk", "c_om"]

        crep = rstd2[:, 0:1024]
        mscr = R1[:, 69632:73728].bitcast(F32)
        modrot = [0]

        MST = [R1[:, 77824:81920].bitcast(F32), R1[:, 81920:86016].bitcast(F32)]
        MSTK = [["LNV", "RSTD"], [("TMP", 0), ("TMP", 1)]]

        def mod_ops(oc_list, early=False):
            for oc in oc_list:
                s = modrot[0] % 2
                modrot[0] += 1
                if early:
                    stg, keys, dk = MST[s], MSTK[s], ("mst", s)
                else:
                    stg, keys, dk = XS[s], [("xs", s)], ("xs", s)
                P.add("sp", lambda e, stg=stg, oc=oc: e.dma_start(out=stg, in_=wada_d[oc]),
                      w=keys, dma=dk)
                P.add("dve", lambda e, stg=stg, oc=oc: e.scalar_tensor_tensor(
                    out=mscr, in0=stg, scalar=1.0, in1=crep, op0=ALU.mult, op1=ALU.mult,
                    accum_out=modT[:, oc:oc + 1]),
                    r=keys + ["crep"], w=[("modc", oc // 24), "SQ"])
                if oc % 24 == 23:
                    sl = oc // 24
                    base = sl * 24
                    P.add("dve", lambda e, base=base: e.tensor_tensor(
                        out=modT[:, base:base + 24], in0=modT[:, base:base + 24], in1=bada[:, base:base + 24],
                        op=ALU.add), r=[("modc", sl)] + CONST, w=[("modc", sl)])
                    gv = vecs[:, sl * 8:(sl + 1) * 8]
                    P.add("dve", lambda e, sl=sl, base=base, gv=gv: e.scalar_tensor_tensor(
                        out=AB[:, sl * 3 + 0, :], in0=modT[:, base + 8:base + 16], scalar=1.0, in1=gv,
                        op0=ALU.add, op1=ALU.mult), r=[("modc", sl)], w=[("AB", sl, 0)])
                    P.add("dve", lambda e, sl=sl, base=base: e.tensor_copy(
                        out=AB[:, sl * 3 + 1, :], in_=modT[:, base:base + 8]), r=[("modc", sl)], w=[("AB", sl, 1)])
                    fac = 1.0 if sl == 1 else 0.5
                    P.add("dve", lambda e, sl=sl, base=base, fac=fac: e.tensor_scalar(
                        out=AB[:, sl * 3 + 2, :], in0=modT[:, base + 16:base + 24], scalar1=fac, scalar2=None,
                        op0=ALU.mult), r=[("modc", sl)], w=[("AB", sl, 2)])

        def phase0_mod():
            cdma(crep, crep_d, "c_crep")
            P.add("act", lambda e: e.activation(out=crep, in_=crep, func=AF.Silu), r=["c_crep"], w=["crep"])
            P.add("dve", lambda e: e.tensor_copy(out=fgv[:], in_=vecs[:, 24:32]), r=CONST, w=["fgv"])
            mod_ops(range(24), early=True)

        mod_rest = list(range(24, 72))

        def load_x(x_ap, Hk, Hv, ntok_blocks, queue="sp", blocks=None):
            for b in (range(ntok_blocks) if blocks is None else blocks):
                s = b % 2
                P.add(queue, lambda e, s=s, b=b: e.dma_start(out=XS[s], in_=x_ap[b * 128:(b + 1) * 128, :]),
                      w=[("xs", s)], dma=("xs", s))
                for half in range(2):
                    pb = ps_next()
                    for q in range(4):
                        kc = half * 4 + q
                        P.add("pe", lambda e, s=s, kc=kc, q=q, pb=pb: e.transpose(
                            psb[pb][:, q * 128:(q + 1) * 128], XS[s][:, kc * 128:(kc + 1) * 128], identF[:]),
                            r=[("xs", s)] + CONST, w=[("ps", pb)])
                    src = psb[pb].rearrange("p (a b) -> p a b", a=4)
                    dst = Hv[:, half * 4:half * 4 + 4, b * 128:(b + 1) * 128]
                    if half == 0:
                        P.add("act", lambda e, src=src, dst=dst: e.copy(out=dst, in_=src),
                              r=[("ps", pb)], w=[Hk])
                    else:
                        P.add("dve", lambda e, src=src, dst=dst: e.tensor_copy(out=dst, in_=src),
                              r=[("ps", pb)], w=[Hk])

        def norm_mod(Hk, Hv, ntiles, sl, Uk, Uv, keep_rstd=None, tile0=0, tiles=None, part=None):
            for tt in (range(ntiles) if tiles is None else tiles):
                cs = slice(tt * 512, (tt + 1) * 512)
                if part in (None, "sq"):
                    P.add("act", lambda e, cs=cs: e.activation(out=SQ[:], in_=Hv[:, :, cs], func=AF.Square),
                          r=[Hk], w=["SQ"])
                if part == "sq":
                    continue
                pb = ps_next()
                for kc in range(KC):
                    P.add("pe", lambda e, kc=kc, pb=pb: e.matmul(
                        psb[pb], lhsT=tri[:, 2, :], rhs=SQ[:, kc, :], start=(kc == 0), stop=(kc == KC - 1)),
                        r=["SQ"] + CONST, w=[("ps", pb)])
                P.add("act", lambda e, pb=pb: e.activation(out=LNV, in_=psb[pb], func=AF.Ln,
                                                          bias=EPS, scale=1.0 / D),
                      r=[("ps", pb)], w=["LNV"])
                if keep_rstd is not None:
                    rs = keep_rstd[:, (tile0 + tt) * 512:(tile0 + tt + 1) * 512]
                    rk = ("rstd2", tile0 + tt)
                    P.add("act", lambda e: e.copy(out=LNV[:, 0:2], in_=LNV[:, 0:2]), r=["LNV"], w=["LNV", "crep"])
                else:
                    rs = RSTD
                    rk = "RSTD"
                P.add("act", lambda e, rs=rs: e.activation(out=rs, in_=LNV, func=AF.Exp, scale=-0.5),
                      r=["LNV"], w=[rk])
                modulate(Hk, Hv, cs, rs, rk, sl, Uk, Uv, cs)

        def modulate(Hk, Hv, hcs, rs, rk, sl, Uk, Uv, ucs):
            for kc in range(KC):
                s = kc % 2
                P.add("dve", lambda e, kc=kc, s=s: e.tensor_tensor(
                    out=TMP[s], in0=Hv[:, kc, hcs], in1=rs, op=ALU.mult),
                    r=[Hk, rk], w=[("TMP", s)])
                P.add("act", lambda e, kc=kc, s=s: e.activation(
                    out=Uv[:, kc, ucs], in_=TMP[s], func=AF.Identity,
                    scale=AB[:, sl * 3 + 0, kc:kc + 1], bias=AB[:, sl * 3 + 1, kc:kc + 1]),
                    r=[("TMP", s), ("AB", sl, 0), ("AB", sl, 1)], w=[Uk])

        def ffn(Hk, Hv, sl, f, hooksA=None, hooksB=None):
            hooksA = hooksA or {}
            hooksB = hooksB or {}
            for fn in hooksA.get(-1, []):
                fn()
            for jp in range(11):
                s = wload(wgu_d[f][jp], 4096)
                wv = wbf[s][:].rearrange("p (j k m) -> p j k m", j=2, k=KC)
                for jj in range(2):
                    j = jp * 2 + jj
                    for tt in range(2):
                        cs = slice(tt * 512, (tt + 1) * 512)
                        pg = ps_next()
                        pu = ps_next()
                        for kc in range(KC):
                            P.add("pe", lambda e, wv=wv, jj=jj, kc=kc, pg=pg, cs=cs: e.matmul(
                                psb[pg], lhsT=wv[:, jj, kc, 0:128], rhs=UT[:, kc, cs],
                                start=(kc == 0), stop=(kc == KC - 1)),
                                r=[("wbf", s), "UT"], w=[("ps", pg)])
                        for kc in range(KC):
                            P.add("pe", lambda e, wv=wv, jj=jj, kc=kc, pu=pu, cs=cs: e.matmul(
                                psb[pu], lhsT=wv[:, jj, kc, 128:256], rhs=UT[:, kc, cs],
                                start=(kc == 0), stop=(kc == KC - 1)),
                                r=[("wbf", s), "UT"], w=[("ps", pu)])
                        sg = (j * 2 + tt) % 2
                        P.add("act", lambda e, pg=pg, sg=sg: e.activation(out=SG[sg], in_=psb[pg], func=AF.Silu),
                              r=[("ps", pg)], w=[("SG", sg)])
                        P.add("dve", lambda e, pu=pu, sg=sg, j=j, cs=cs: e.tensor_tensor(
                            out=ACTB[:, j, cs], in0=SG[sg], in1=psb[pu], op=ALU.mult),
                            r=[("ps", pu), ("SG", sg)], w=[("ACTB", j, tt)])
                if mod_rest:
                    take, mod_rest[:] = mod_rest[:5], mod_rest[5:]
                    mod_ops(take)
                for fn in hooksA.get(jp, []):
                    fn()
            for dc in range(KC):
                s = wload(wd_d[f][dc], DFF)
                wv = wbf[s][:, 0:DFF].rearrange("p (j m) -> p j m", j=NJ)
                for tt in range(2):
                    cs = slice(tt * 512, (tt + 1) * 512)
                    py = ps_next()
                    for jc in range(NJ):
                        P.add("pe", lambda e, wv=wv, jc=jc, py=py, cs=cs: e.matmul(
                            psb[py], lhsT=wv[:, jc, :], rhs=ACTB[:, jc, cs],
                            start=(jc == 0), stop=(jc == NJ - 1)),
                            r=[("wbf", s), ("ACTB", jc, tt)], w=[("ps", py)])
                    P.add("dve", lambda e, dc=dc, py=py, cs=cs: e.scalar_tensor_tensor(
                        out=Hv[:, dc, cs], in0=psb[py], scalar=AB[:, sl * 3 + 2, dc:dc + 1],
                        in1=Hv[:, dc, cs], op0=ALU.mult, op1=ALU.add),
                        r=[("ps", py), ("AB", sl, 2)], w=[Hk])
                for fn in hooksB.get(dc, []):
                    fn()

        def proj_T(widx, n_chunks_lo, Uk, dst_fn, scale, tag):
            s = wload(wmix_d[widx], 4096)
            wv = wbf[s][:].rearrange("p (k n) -> p k n", k=KC)
            for c in range(4):
                for tt in range(2):
                    cs = slice(tt * 512, (tt + 1) * 512)
                    pb = ps_next()
                    for kc in range(KC):
                        P.add("pe", lambda e, wv=wv, c=c, kc=kc, pb=pb, cs=cs: e.matmul(
                            psb[pb], lhsT=wv[:, kc, c * 128:(c + 1) * 128], rhs=UT2[:, kc, cs],
                            start=(kc == 0), stop=(kc == KC - 1)),
                            r=[("wbf", s), Uk], w=[("ps", pb)])
                    dst_fn(c, tt, pb)

        qsrot = [0]

        def st_info(sidx):
            is_own = sidx >= 2
            st = sidx % 2
            Hk = ("H", st)
            Hv = hT[:, :, st * 1024:(st + 1) * 1024]
            x_ap = (xo_d if is_own else xp_d)[st * 1024:(st + 1) * 1024, :]
            return is_own, st, Hk, Hv, x_ap

        def pre_load(sidx, queue="sp", blocks=None):
            is_own, st, Hk, Hv, x_ap = st_info(sidx)
            load_x(x_ap, Hk, Hv, 8, queue, blocks)

        def pre_norm(sidx, tt, part=None):
            is_own, st, Hk, Hv, x_ap = st_info(sidx)
            norm_mod(Hk, Hv, 2, 0, "UT", UT, tiles=[tt], part=part)

        def n2(sidx, tt, part=None):
            is_own, st, Hk, Hv, x_ap = st_info(sidx)
            norm_mod(Hk, Hv, 2, 1, "UT2", UT2, keep_rstd=(rstd2 if is_own else None), tile0=st * 2, tiles=[tt],
                     part=part)

        def qkv(sidx):
            is_own, st, Hk, Hv, x_ap = st_info(sidx)
            par = 1 if is_own else 0

            def evac_to_dram(dram_fn, scale):
                def f(c, tt, pb):
                    q = qsrot[0] % 2
                    qsrot[0] += 1
                    P.add("act", lambda e, pb=pb, q=q: e.activation(
                        out=QS[q], in_=psb[pb], func=AF.Identity, scale=scale),
                        r=[("ps", pb)], w=[("QS", q)])
                    dst = dram_fn(c, tt)
                    src = QS[q] if len(dst.shape) == 2 else QS[q].rearrange("p (a b) -> p a b", a=4)
                    P.add("sp", lambda e, dst=dst, src=src: e.dma_start(out=dst, in_=src),
                          r=[("QS", q)], dma=("qs", q))
                return f

            if is_own:
                proj_T(3, 4, "UT2", evac_to_dram(
                    lambda c, tt: qd_d[c][:, st * 1024 + tt * 512: st * 1024 + (tt + 1) * 512], 0.125), 0.125, "q")
            i0f = lambda tt: st * 8 + tt * 4
            proj_T(4, 4, "UT2", evac_to_dram(
                lambda c, tt: kd_d[c][:, 2 * i0f(tt) + par: 2 * i0f(tt) + par + 7: 2, :], 1.0), 1.0, "k")
            s = wload(wmix_d[5], 4096)
            wv = wbf[s][:].rearrange("p (k n) -> p k n", k=KC)
            for b in range(8):
                pb = ps_next()
                for kc in range(KC):
                    P.add("pe", lambda e, wv=wv, kc=kc, pb=pb, b=b: e.matmul(
                        psb[pb], lhsT=UT2[:, kc, b * 128:(b + 1) * 128], rhs=wv[:, kc, :],
                        start=(kc == 0), stop=(kc == KC - 1)),
                        r=[("wbf", s), "UT2"], w=[("ps", pb)])
                q = qsrot[0] % 2
                qsrot[0] += 1
                P.add("act", lambda e, pb=pb, q=q: e.copy(out=QS[q], in_=psb[pb]),
                      r=[("ps", pb)], w=[("QS", q)])
                sig = 2 * (st * 8 + b) + par
                P.add("sp", lambda e, q=q, sig=sig: e.dma_start(out=vd_d[sig], in_=QS[q]),
                      r=[("QS", q)], dma=("qs", q))
            if not is_own:
                ucols = UT2[:, :, :].rearrange("p k (b t) -> p k b t", t=128)
                pcs = []
                for widx in (1, 2):
                    s = wload(wmix_d[widx], 4096)
                    wv = wbf[s][:].rearrange("p (k n) -> p k n", k=KC)
                    pb = ps_next()
                    for c in range(4):
                        for kc in range(KC):
                            P.add("pe", lambda e, wv=wv, c=c, kc=kc, pb=pb: e.matmul(
                                psb[pb][:, c * 16:(c + 1) * 16].rearrange("p (b t) -> p b t", t=2),
                                lhsT=wv[:, kc, c * 128:(c + 1) * 128], rhs=ucols[:, kc, :, 126:128],
                                start=(kc == 0), stop=(kc == KC - 1)),
                                r=[("wbf", s), "UT2"], w=[("ps", pb)])
                    pcs.append(pb)
                P.add("act", lambda e, pb=pcs[0]: e.copy(out=TMP[0][:, 0:64], in_=psb[pb][:, 0:64]),
                      r=[("ps", pcs[0])], w=[("TMP", 0)])
                P.add("dve", lambda e, pb=pcs[1], st=st: e.tensor_tensor(
                    out=halo[:, :, st * 8:(st + 1) * 8, :],
                    in0=TMP[0][:, 0:64].rearrange("p (c b t) -> p c b t", c=4, t=2),
                    in1=psb[pb][:, 0:64].rearrange("p (c b t) -> p c b t", c=4, t=2), op=ALU.mult),
                    r=[("ps", pcs[1]), ("TMP", 0)], w=["halo"])
                if st == 0:
                    P.add("dve", lambda e: e.tensor_scalar(
                        out=halo[:, :, 0, :], in0=halo[:, :, 0, :], scalar1=om[:, 0:1], scalar2=None,
                        op0=ALU.mult), r=["halo"] + CONST, w=["halo"])

        phase0_mod()
        pre_load(0, "pool")
        pre_norm(0, 0)
        pre_norm(0, 1)
        for sidx in range(4):
            hA, hB = {}, {}
            if sidx > 0:
                hA = {-1: [lambda p=sidx - 1: n2(p, 0, "sq")],
                      0: [lambda p=sidx - 1: n2(p, 0, "rest")],
                      1: [lambda p=sidx - 1: n2(p, 1, "sq")],
                      2: [lambda p=sidx - 1: n2(p, 1, "rest")],
                      4: [lambda p=sidx - 1: qkv(p)]}
            if sidx < 3:
                hB = {0: [lambda n=sidx + 1: pre_load(n, "sp", [0, 1])],
                      1: [lambda n=sidx + 1: pre_load(n, "sp", [2, 3])],
                      2: [lambda n=sidx + 1: pre_load(n, "sp", [4, 5]), lambda n=sidx + 1: pre_norm(n, 0, "sq")],
                      3: [lambda n=sidx + 1: pre_load(n, "sp", [6, 7]), lambda n=sidx + 1: pre_norm(n, 0, "rest")],
                      4: [lambda n=sidx + 1: pre_norm(n, 1, "sq")],
                      5: [lambda n=sidx + 1: pre_norm(n, 1, "rest")]}
            is_own, st, Hk, Hv, x_ap = st_info(sidx)
            ffn(Hk, Hv, 0, 0, hA, hB)
        n2(3, 0)
        n2(3, 1)
        qkv(3)

        if DEBUG_TAP == "h1":
            P.add("sp", lambda e: e.dma_start(out=dbg_d, in_=hT[:]), r=[("H", 0), ("H", 1)], w=["dbg"], dma="dbg")

        P.barrier()
        off[0] = 0
        OT = r3(carve(16384, BF16), 4)
        kt = carve(8192, BF16).rearrange("p (s t) -> p s t", s=32)
        ktn = carve(8192, BF16).rearrange("p (s t) -> p s t", s=32)
        QM = carve(8192, BF16).rearrange("p (h t) -> p h t", h=2)
        VM = carve(16384, BF16).rearrange("p (h s d) -> p h s d", h=2, s=32)
        EB2 = [carve(4096, F32) for _ in range(2)]
        SP2 = [carve(2048, BF16) for _ in range(3)]
        AH2 = [carve(2048, BF16) for _ in range(2)]

        kvk = "kv"
        P.add("dve", lambda e: e.memset(QM[64:128, 0, :], 0.0), w=[(kvk, "q")])
        P.add("dve", lambda e: e.memset(QM[0:64, 1, :], 0.0), w=[(kvk, "q")])
        P.add("dve", lambda e: e.memset(VM[:, 0, :, 64:128], 0.0), w=[(kvk, "v")])
        P.add("dve", lambda e: e.memset(VM[:, 1, :, 0:64], 0.0), w=[(kvk, "v")])
        for hp in range(4):
            P.add("sp", lambda e, hp=hp: e.dma_start(out=kt, in_=kd_d[hp]),
                  w=[(kvk, "k")], dma=(kvk, "k"))
            P.add("sp", lambda e, hp=hp: e.dma_start(out=QM[0:64, 0, :], in_=qd_d[hp][0:64, :]),
                  w=[(kvk, "q")], dma=(kvk, "q"))
            P.add("sp", lambda e, hp=hp: e.dma_start(out=QM[64:128, 1, :], in_=qd_d[hp][64:128, :]),
                  w=[(kvk, "q")], dma=(kvk, "q"))
            for h in range(2):
                P.add("sp", lambda e, hp=hp, h=h: e.dma_start(
                    out=VM[:, h, :, 64 * h:64 * h + 64],
                    in_=vd_d[:, :, hp * 128 + 64 * h:hp * 128 + 64 * h + 64].rearrange("s p d -> p s d")),
                    w=[(kvk, "v")], dma=(kvk, "v"))
            P.add("dve", lambda e: e.tensor_scalar(
                out=ktn, in0=kt, scalar1=-1.0, scalar2=None, op0=ALU.mult),
                r=[(kvk, "k")], w=[(kvk, "kn")])
            for qtile in range(4):
                nkb = 8 * qtile + 8
                po = 6 + (qtile % 2)
                cs = slice(qtile * 512, (qtile + 1) * 512)

                qs = qtile * 512

                def c0_of(m):
                    return max(0, 3 - m // 2) * 128 if m < 8 else 0

                def v3(ap):
                    return ap.rearrange("p (h t) -> p h t", h=2)

                def masks(m, target_fn, key):
                    sig = nkb - 1 - m
                    c0 = c0_of(m)
                    if m < 8 and m % 2 == 0:
                        for h in range(2):
                            tg = target_fn(h)[:, c0:c0 + 128]
                            P.add("dve", lambda e, tg=tg: e.tensor_tensor(out=tg, in0=tg, in1=mk[:, 0, :], op=ALU.mult),
                                  r=[key] + CONST, w=[key])
                    if sig == 0:
                        for h in range(2):
                            tg = target_fn(h)[:, c0:512]
                            P.add("dve", lambda e, tg=tg: e.tensor_scalar(
                                out=tg, in0=tg, scalar1=om[:, 0:1], scalar2=None, op0=ALU.mult),
                                r=[key] + CONST, w=[key])

                def Zp(m):
                    sig = nkb - 1 - m
                    par = m % 2
                    c0 = c0_of(m)
                    for h in range(2):
                        pz = par * 2 + h
                        P.add("pe", lambda e, pz=pz, sig=sig, h=h, c0=c0, qs=qs: e.matmul(
                            psb[pz][:, c0:512], lhsT=kt[:, sig, :], rhs=QM[:, h, qs + c0:qs + 512],
                            start=True, stop=True),
                            r=[(kvk, "k"), (kvk, "q")], w=[("zp", par)])

                def S1a(m):
                    par = m % 2
                    sp3 = m % 3
                    c0 = c0_of(m)
                    P.add("act", lambda e, par=par, c0=c0: e.activation(
                        out=v3(EB2[par])[:, :, c0:512], in_=v3(psp[par][:])[:, :, c0:512], func=AF.Exp),
                        r=[("zp", par)], w=[("EB2", par)])
                    masks(m, lambda h, par=par: EB2[par][:, h * 512:(h + 1) * 512], ("EB2", par))
                    P.add("act", lambda e, par=par, sp3=sp3, c0=c0: e.activation(
                        out=v3(SP2[sp3])[:, :, c0:512], in_=v3(EB2[par])[:, :, c0:512], func=AF.Ln,
                        bias=1.0, scale=1.0),
                        r=[("EB2", par)], w=[("SP2", sp3)])

                def BZL(m):
                    sig = nkb - 1 - m
                    sp3 = m % 3
                    c0 = c0_of(m)
                    for h in range(2):
                        pbk = 4 + h
                        sp = SP2[sp3][:, h * 512 + c0:(h + 1) * 512]
                        P.add("pe", lambda e, h=h, pbk=pbk, sig=sig, m=m, c0=c0, qs=qs: e.matmul(
                            psb[pbk][:, c0:512], lhsT=kt[:, sig, :], rhs=QM[:, h, qs + c0:qs + 512],
                            start=(m == 0), stop=False, skip_group_check=True),
                            r=[(kvk, "k"), (kvk, "q")], w=["BB"])
                        P.add("pe", lambda e, pbk=pbk, sp=sp, c0=c0: e.matmul(
                            psb[pbk][:, c0:512], lhsT=tri[:, 0, :], rhs=sp, start=False, stop=False,
                            skip_group_check=True),
                            r=[("SP2", sp3)] + CONST, w=["BB"])

                def A2(m):
                    par = m % 2
                    c0 = c0_of(m)
                    P.add("act", lambda e, par=par, c0=c0: e.activation(
                        out=v3(AH2[par])[:, :, c0:512], in_=v3(psp[2][:])[:, :, c0:512], func=AF.Exp),
                        r=["BB"], w=[("AH2", par)])
                    masks(m, lambda h, par=par: AH2[par][:, h * 512:(h + 1) * 512], ("AH2", par))

                def UZ(m):
                    sig = nkb - 1 - m
                    sp3 = m % 3
                    c0 = c0_of(m)
                    for h in range(2):
                        pbk = 4 + h
                        sp = SP2[sp3][:, h * 512 + c0:(h + 1) * 512]
                        P.add("pe", lambda e, pbk=pbk, sp=sp, c0=c0: e.matmul(
                            psb[pbk][:, c0:512], lhsT=tri[:, 1, :], rhs=sp, start=False, stop=False,
                            skip_group_check=True),
                            r=[("SP2", sp3)] + CONST, w=["BB"])
                        P.add("pe", lambda e, h=h, pbk=pbk, sig=sig, m=m, c0=c0, qs=qs, nkb=nkb: e.matmul(
                            psb[pbk][:, c0:512], lhsT=ktn[:, sig, :], rhs=QM[:, h, qs + c0:qs + 512],
                            start=False, stop=(m == nkb - 1), skip_group_check=True),
                            r=[(kvk, "kn"), (kvk, "q")], w=["BB"])

                def AV(m):
                    sig = nkb - 1 - m
                    par = m % 2
                    c0 = c0_of(m)
                    for h in range(2):
                        ah = AH2[par][:, h * 512 + c0:(h + 1) * 512]
                        P.add("pe", lambda e, h=h, sig=sig, ah=ah, m=m, po=po, nkb=nkb, c0=c0: e.matmul(
                            psb[po][:, c0:512], lhsT=VM[:, h, sig, :], rhs=ah, start=(m == 0 and h == 0),
                            stop=(m == nkb - 1 and h == 1), skip_group_check=True),
                            r=[(kvk, "v"), ("AH2", par)], w=[("ps", po)])

                def S2(m, h):
                    sig = nkb - 1 - m
                    par = m % 2
                    c0 = c0_of(m)
                    pbk = 4 + h
                    Bk = ("ps", pbk)
                    sp = SP2[par][:, h * 512 + c0:(h + 1) * 512]
                    P.add("pe", lambda e, h=h, pbk=pbk, sig=sig, m=m, c0=c0, qs=qs: e.matmul(
                        psb[pbk][:, c0:512], lhsT=kt[:, sig, :], rhs=QM[:, h, qs + c0:qs + 512],
                        start=(m == 0), stop=False, skip_group_check=True),
                        r=[(kvk, "k"), (kvk, "q")], w=[Bk])
                    P.add("pe", lambda e, pbk=pbk, sp=sp, c0=c0: e.matmul(
                        psb[pbk][:, c0:512], lhsT=tri[:, 0, :], rhs=sp, start=False, stop=False,
                        skip_group_check=True),
                        r=[("SP2", par)] + CONST, w=[Bk])
                    ah = AH[h][par]
                    P.add("act", lambda e, pbk=pbk, ah=ah, c0=c0: e.activation(
                        out=ah[:, c0:512], in_=psb[pbk][:, c0:512], func=AF.Exp),
                        r=[Bk], w=[("AH", h, par)])

                def S2m(m):
                    par = m % 2
                    for h in range(2):
                        masks_h(m, h, AH[h][par], ("AH", h, par))

                def masks_h(m, h, target, key):
                    sig = nkb - 1 - m
                    c0 = c0_of(m)
                    if m < 8 and m % 2 == 0:
                        tg = target[:, c0:c0 + 128]
                        P.add("dve", lambda e, tg=tg: e.tensor_tensor(out=tg, in0=tg, in1=mk[:, 0, :], op=ALU.mult),
                              r=[key] + CONST, w=[key])
                    if sig == 0:
                        tg = target[:, c0:512]
                        P.add("dve", lambda e, tg=tg: e.tensor_scalar(
                            out=tg, in0=tg, scalar1=om[:, 0:1], scalar2=None, op0=ALU.mult),
                            r=[key] + CONST, w=[key])

                def S3(m, h):
                    sig = nkb - 1 - m
                    par = m % 2
                    c0 = c0_of(m)
                    pbk = 4 + h
                    Bk = ("ps", pbk)
                    sp = SP2[par][:, h * 512 + c0:(h + 1) * 512]
                    ah = AH[h][par]
                    P.add("pe", lambda e, pbk=pbk, sp=sp, c0=c0: e.matmul(
                        psb[pbk][:, c0:512], lhsT=tri[:, 1, :], rhs=sp, start=False, stop=False,
                        skip_group_check=True),
                        r=[("SP2", par)] + CONST, w=[Bk])
                    P.add("pe", lambda e, h=h, pbk=pbk, sig=sig, m=m, c0=c0, qs=qs, nkb=nkb: e.matmul(
                        psb[pbk][:, c0:512], lhsT=ktn[:, sig, :], rhs=QM[:, h, qs + c0:qs + 512],
                        start=False, stop=(m == nkb - 1), skip_group_check=True),
                        r=[(kvk, "kn"), (kvk, "q")], w=[Bk])
                    P.add("pe", lambda e, h=h, sig=sig, ah=ah, m=m, po=po, nkb=nkb, c0=c0: e.matmul(
                        psb[po][:, c0:512], lhsT=VM[:, h, sig, :], rhs=ah[:, c0:512], start=(m == 0 and h == 0),
                        stop=(m == nkb - 1 and h == 1), skip_group_check=True),
                        r=[(kvk, "v"), ("AH", h, par)], w=[("ps", po)])

                Zp(0)
                Zp(1)
                S1a(0)
                Zp(2)
                S1a(1)
                BZL(0)
                for m in range(nkb):
                    A2(m)
                    UZ(m)
                    if m + 1 < nkb:
                        BZL(m + 1)
                    AV(m)
                    if m + 3 < nkb:
                        Zp(m + 3)
                    if m + 2 < nkb:
                        S1a(m + 2)
                P.add("dve", lambda e, po=po, hp=hp, cs=cs: e.tensor_copy(out=OT[:, hp, cs], in_=psb[po]),
                      r=[("ps", po)], w=[("OT", hp)])

        P.barrier()
        off[0] = 16384
        U2B = [r3(carve(8192, BF16), KC) for _ in range(2)]
        VC = carve(8320, F32).rearrange("p (c b t) -> p c b t", c=4, b=4)
        CV = r3(carve(8192, F32), 4)
        YAI = r3(carve(4096, BF16), 4)
        SGA = r3(carve(16384, F32), KC)
        SGB = r3(carve(16384, F32), KC)
        M1 = SGA
        MG = r3(carve(8192, BF16), KC)
        TM2 = [carve(2048, F32) for _ in range(2)]
        TM3 = [carve(2048, F32) for _ in range(2)]

        def wview(s):
            return wbf[s][:].rearrange("p (k n) -> p k n", k=KC)

        def u2_recompute(tt):
            gcs = slice(tt * 512, (tt + 1) * 512)
            Hk = ("H", tt // 2)
            U2 = U2B[tt % 2]
            for kc in range(KC):
                s2 = kc % 2
                P.add("dve", lambda e, kc=kc, s2=s2, gcs=gcs: e.tensor_tensor(
                    out=TM2[s2], in0=hT[:, kc, gcs], in1=rstd2[:, gcs], op=ALU.mult),
                    r=[Hk, ("rstd2", tt)], w=[("TM2", s2)])
                P.add("act", lambda e, kc=kc, s2=s2, U2=U2: e.activation(
                    out=U2[:, kc, :], in_=TM2[s2], func=AF.Identity,
                    scale=AB[:, 3, kc:kc + 1], bias=AB[:, 4, kc:kc + 1]),
                    r=[("TM2", s2), ("AB", 1, 0), ("AB", 1, 1)], w=[("U2", tt % 2)])

        u2_recompute(0)
        for tt in range(4):
            gcs = slice(tt * 512, (tt + 1) * 512)
            Hk = ("H", tt // 2)
            U2 = U2B[tt % 2]
            U2k = ("U2", tt % 2)
            P.add("dve", lambda e, tt=tt: e.tensor_copy(out=VC[:, :, :, 0:2], in_=halo[:, :, tt * 4:(tt + 1) * 4, :]),
                  r=["halo"], w=["VC"])

            def proj_tile(widx, fn, U2=U2, U2k=U2k):
                s = wload(wmix_d[widx], 4096)
                wv = wview(s)
                for c in range(4):
                    pb = ps_next()
                    for kc in range(KC):
                        P.add("pe", lambda e, wv=wv, c=c, kc=kc, pb=pb, U2=U2: e.matmul(
                            psb[pb], lhsT=wv[:, kc, c * 128:(c + 1) * 128], rhs=U2[:, kc, :],
                            start=(kc == 0), stop=(kc == KC - 1)),
                            r=[("wbf", s), U2k], w=[("ps", pb)])
                    fn(c, pb)

            def f_cc(c, pb):
                P.add("act", lambda e, c=c, pb=pb: e.copy(out=CV[:, c, :], in_=psb[pb]),
                      r=[("ps", pb)], w=[("CV", c)])
            proj_tile(1, f_cc)

            def f_cx(c, pb):
                P.add("dve", lambda e, c=c, pb=pb: e.tensor_tensor(
                    out=VC[:, c, :, 2:130], in0=CV[:, c, :].rearrange("p (b t) -> p b t", b=4),
                    in1=psb[pb].rearrange("p (b t) -> p b t", b=4), op=ALU.mult),
                    r=[("ps", pb), ("CV", c)], w=["VC"])
            proj_tile(2, f_cx)
            for c in range(4):
                cvv = CV[:, c, :].rearrange("p (b t) -> p b t", b=4)
                P.add("dve", lambda e, c=c, cvv=cvv: e.tensor_scalar(
                    out=cvv, in0=VC[:, c, :, 0:128], scalar1=convw[:, c * 3:c * 3 + 1], scalar2=None, op0=ALU.mult),
                    r=["VC", ("CV", c)] + CONST, w=[("CV", c)])
                for k in (1, 2):
                    P.add("dve", lambda e, c=c, k=k, cvv=cvv: e.scalar_tensor_tensor(
                        out=cvv, in0=VC[:, c, :, k:k + 128], scalar=convw[:, c * 3 + k:c * 3 + k + 1],
                        in1=cvv, op0=ALU.mult, op1=ALU.add),
                        r=["VC", ("CV", c)] + CONST, w=[("CV", c)])

            def gates(gidx0, SGX, bm_off, tag):
                for gp in range(2):
                    def f_g(c, pb, gp=gp):
                        oc = gp * 4 + c
                        P.add("act", lambda e, pb=pb, oc=oc: e.activation(
                            out=SGX[:, oc, :], in_=psb[pb], func=AF.Sigmoid,
                            bias=vecs[:, bm_off + oc:bm_off + oc + 1], scale=1.0),
                            r=[("ps", pb)] + CONST, w=[(tag, oc)])
                    proj_tile(gidx0 + gp, f_g)
            gates(6, SGA, 32, "SGA")
            gates(8, SGB, 40, "SGB")
            if tt + 1 < 4:
                u2_recompute(tt + 1)

            def f_cb(c, pb):
                P.add("dve", lambda e, c=c, pb=pb: e.tensor_tensor(
                    out=YAI[:, c, :], in0=CV[:, c, :], in1=psb[pb], op=ALU.mult),
                    r=[("ps", pb), ("CV", c)], w=[("YAI", c)])
            proj_tile(0, f_cb)

            def branch(wsrc, rhs_fn, rhs_keys, first):
                sw = wload(wsrc, 4096)
                wvo = wbf[sw][:].rearrange("p (o c m) -> p o c m", o=KC, c=4)
                for oc in range(KC):
                    py = ps_next()
                    for cc in range(4):
                        P.add("pe", lambda e, wvo=wvo, oc=oc, cc=cc, py=py: e.matmul(
                            psb[py], lhsT=wvo[:, oc, cc, :], rhs=rhs_fn(cc),
                            start=(cc == 0), stop=(cc == 3)),
                            r=[("wbf", sw)] + rhs_keys, w=[("ps", py)])
                    if first:
                        P.add("dve", lambda e, py=py, oc=oc: e.tensor_tensor(
                            out=M1[:, oc, :], in0=SGA[:, oc, :], in1=psb[py], op=ALU.mult),
                            r=[("ps", py), ("SGA", oc)], w=[("SGA", oc)])
                    else:
                        s2 = oc % 2
                        P.add("dve", lambda e, py=py, s2=s2, oc=oc: e.tensor_tensor(
                            out=TM3[s2], in0=SGB[:, oc, :], in1=psb[py], op=ALU.mult),
                            r=[("ps", py), ("SGB", oc)], w=[("TM3", s2)])
                        P.add("dve", lambda e, s2=s2, oc=oc: e.tensor_tensor(
                            out=MG[:, oc, :], in0=TM3[s2], in1=M1[:, oc, :], op=ALU.add),
                            r=[("TM3", s2), ("SGA", oc)], w=[("MG", oc)])

            branch(wco_d, lambda cc: YAI[:, cc, :], [("YAI", c) for c in range(4)], True)
            branch(wao_d, lambda cc, gcs=gcs: OT[:, cc, gcs], [("OT", c) for c in range(4)], False)
            for pc in range(2):
                s = wload(wo_d[pc], 4096)
                wv = wbf[s][:].rearrange("p (o k m) -> p o k m", o=4, k=KC)
                for o in range(4):
                    oc = pc * 4 + o
                    pb = ps_next()
                    for kc in range(KC):
                        P.add("pe", lambda e, wv=wv, o=o, kc=kc, pb=pb: e.matmul(
                            psb[pb], lhsT=wv[:, o, kc, :], rhs=MG[:, kc, :],
                            start=(kc == 0), stop=(kc == KC - 1)),
                            r=[("wbf", s)] + [("MG", k) for k in range(KC)], w=[("ps", pb)])
                    P.add("dve", lambda e, oc=oc, pb=pb, gcs=gcs: e.scalar_tensor_tensor(
                        out=hT[:, oc, gcs], in0=psb[pb], scalar=AB[:, 5, oc:oc + 1],
                        in1=hT[:, oc, gcs], op0=ALU.mult, op1=ALU.add),
                        r=[("ps", pb), ("AB", 1, 2)], w=[Hk])

        if DEBUG_TAP == "h2":
            P.add("sp", lambda e: e.dma_start(out=dbg_d, in_=hT[:]), r=[("H", 0), ("H", 1)], w=["dbg"], dma="dbg")

        P.barrier()
        OUTT = r3(R1[:, 0:16384].bitcast(F32), KC)

        def final_tile(tt):
            gcs = slice(tt * 512, (tt + 1) * 512)
            Hk = ("H", tt // 2)
            P.add("act", lambda e, gcs=gcs: e.activation(out=SQ[:], in_=hT[:, :, gcs], func=AF.Square),
                  r=[Hk], w=["SQ"])
            pb = ps_next()
            for kc in range(KC):
                P.add("pe", lambda e, kc=kc, pb=pb: e.matmul(
                    psb[pb], lhsT=tri[:, 2, :], rhs=SQ[:, kc, :], start=(kc == 0), stop=(kc == KC - 1)),
                    r=["SQ"] + CONST, w=[("ps", pb)])
            P.add("act", lambda e, pb=pb: e.activation(out=LNV, in_=psb[pb], func=AF.Ln, bias=EPS, scale=1.0 / D),
                  r=[("ps", pb)], w=["LNV"])
            P.add("act", lambda e: e.activation(out=RSTD, in_=LNV, func=AF.Exp, scale=-0.5), r=["LNV"], w=["RSTD"])
            for kc in range(KC):
                P.add("dve", lambda e, kc=kc, gcs=gcs: e.scalar_tensor_tensor(
                    out=OUTT[:, kc, :], in0=hT[:, kc, gcs], scalar=fgv[:, kc:kc + 1], in1=RSTD,
                    op0=ALU.mult, op1=ALU.mult), r=[Hk, "RSTD", "fgv"], w=[("OUTT", kc), "UT"])
            for b in range(4):
                s = (tt * 4 + b) % 2
                for half in range(2):
                    pb = ps_next()
                    for q in range(4):
                        kc = half * 4 + q
                        P.add("pe", lambda e, kc=kc, q=q, pb=pb, b=b: e.transpose(
                            psb[pb][:, q * 128:(q + 1) * 128], OUTT[:, kc, b * 128:(b + 1) * 128], identF[:]),
                            r=[("OUTT", kc), "UT"] + CONST, w=[("ps", pb)])
                    if half == 0:
                        P.add("act", lambda e, pb=pb, s=s: e.copy(out=XS[s][:, 0:512], in_=psb[pb]),
                              r=[("ps", pb)], w=[("xs", s)])
                    else:
                        P.add("dve", lambda e, pb=pb, s=s: e.tensor_copy(out=XS[s][:, 512:1024], in_=psb[pb]),
                              r=[("ps", pb)], w=[("xs", s)])
                row = (tt * 4 + b) * 128
                P.add("sp", lambda e, s=s, row=row: e.dma_start(out=y_d[row:row + 128, :], in_=XS[s]),
                      r=[("xs", s)], w=["ydram"], dma=("xs", s))

        def n3(st, tt):
            norm_mod(("H", st), hT[:, :, st * 1024:(st + 1) * 1024], 2, 2, "UT", UT, tiles=[tt])

        n3(0, 0)
        n3(0, 1)
        ffn(("H", 0), hT[:, :, 0:1024], 2, 1, None, {3: [lambda: n3(1, 0)], 5: [lambda: n3(1, 1)]})
        ffn(("H", 1), hT[:, :, 1024:2048], 2, 1, None, {1: [lambda: final_tile(0)], 4: [lambda: final_tile(1)]})
        final_tile(2)
        final_tile(3)

        P.emit(nc, block, sems)
    return nc, P


def _host_layouts(inp):
    f = np.float32
    g = {}
    w = np.asarray(inp["w_ada"][0], f)
    g["wada"] = np.ascontiguousarray(w.T).reshape(72, 128, 1024)
    for n, (kgu, kd) in enumerate((("ffn1_w_gu", "ffn1_w_down"), ("ffn2_w_gu", "ffn2_w_down"))):
        W = np.asarray(inp[kgu][0], f)
        Wg = W[:, :DFF].reshape(8, 128, 11, 2, 128)
        Wu = W[:, DFF:].reshape(8, 128, 11, 2, 128)
        S = np.stack([Wg, Wu], axis=4)
        g[f"wgu{n + 1}"] = np.ascontiguousarray(S.transpose(2, 1, 3, 0, 4, 5)).reshape(11, 128, 4096)
        Wd = np.asarray(inp[kd][0], f).reshape(NJ, 128, 8, 128)
        g[f"wd{n + 1}"] = np.ascontiguousarray(Wd.transpose(2, 1, 0, 3)).reshape(8, 128, DFF)
    W = np.asarray(inp["w_mix_in"][0], f)
    g["wmix"] = np.ascontiguousarray(W.reshape(8, 128, 10, 512).transpose(2, 1, 0, 3)).reshape(10, 128, 4096)
    for k, n in (("w_conv_out", "wco"), ("w_attn_out", "wao")):
        W = np.asarray(inp[k][0], f)
        g[n] = np.ascontiguousarray(W.reshape(4, 128, 8, 128).transpose(1, 2, 0, 3)).reshape(128, 4096)
    W = np.asarray(inp["w_out"][0], f)
    g["wo"] = np.ascontiguousarray(W.reshape(8, 128, 2, 4, 128).transpose(2, 1, 3, 0, 4)).reshape(2, 128, 4096)

    def v8(a):
        return np.asarray(a, f).reshape(8, 128).T

    g["vecs"] = np.ascontiguousarray(np.concatenate([
        v8(inp["norm1_g"][0]), v8(inp["norm2_g"][0]), v8(inp["norm3_g"][0]), v8(inp["final_g"]),
        v8(inp["b_merge"][0, 0]), v8(inp["b_merge"][0, 1])], axis=1))
    g["bada"] = np.ascontiguousarray(np.asarray(inp["b_ada"][0], f).reshape(72, 128).T)
    g["convw"] = np.ascontiguousarray(np.asarray(inp["conv_w"][0], f).reshape(3, 4, 128).transpose(2, 1, 0)).reshape(128, 12)
    g["ident"] = np.eye(128, dtype=f)
    jj = np.arange(128)[:, None]
    ss = np.arange(128)[None, :]
    tri = np.zeros((128, 3, 128), f)
    tri[:, 0, :] = -(jj >= ss).astype(f)
    tri[:, 1, :] = -(jj < ss).astype(f)
    tri[:, 2, :] = 1.0
    g["tri"] = tri
    return g


def _masks(r):
    f = np.float32
    s_ = np.arange(128)[:, None]
    t_ = np.arange(128)[None, :]
    trim = (s_ < t_).astype(f)
    mk = np.ascontiguousarray(np.stack([trim, trim], axis=1))
    om = np.full((128, 1), 1.0 if r == 1 else 0.0, f)
    return mk, om


_CACHE = {}


def kernel(**inputs):
    x = np.asarray(inputs["x"], np.float32)
    c = np.asarray(inputs["c"], np.float32)
    if "nc" not in _CACHE:
        _CACHE["nc"] = build_program()
    nc, P = _CACHE["nc"]
    g = _host_layouts(inputs)
    in_maps = []
    for core in range(8):
        b, r = core // 2, core % 2
        xb = x[b].reshape(32, 128, D)
        xo = xb[r::2].reshape(T, D)
        if r == 1:
            xp = xb[0::2].reshape(T, D)
        else:
            xp = np.concatenate([np.zeros((1, 128, D), np.float32), xb[1::2][:15]], 0).reshape(T, D)
        m = dict(g)
        m["xo"] = np.ascontiguousarray(xo)
        m["xp"] = np.ascontiguousarray(xp)
        m["crep"] = np.ascontiguousarray(np.broadcast_to(c[b][None, :], (128, D)))
        m["mk"], m["om"] = _masks(r)
        in_maps.append(m)
    res = run_bass_kernel_spmd(nc, in_maps, core_ids=list(range(8)))
    out = np.zeros((4, 32, 128, D), np.float32)
    for core in range(8):
        b, r = core // 2, core % 2
        out[b, r::2] = np.asarray(res.results[core]["y"], np.float32).reshape(16, 128, D)
    _CACHE["last"] = res
    return out.reshape(4, 4096, D)
```

```python
import numpy as np
import concourse.bass as bass
import concourse.mybir as mybir
from concourse.bass_utils import run_bass_kernel_spmd

F32 = mybir.dt.float32
BF16 = mybir.dt.bfloat16
U8 = mybir.dt.uint8
AF = mybir.ActivationFunctionType
ALU = mybir.AluOpType

D = 1024
KC = 8
DFF = 2816
NJ = 22
T = 2048
NBLK = 16
EPS = 1e-6
SAME_ENG_SYNC = True
DEBUG_TAP = None


class Prog:
    def __init__(self):
        self.ops = []
        self.lastw = {}
        self.readers = {}
        self.dma_cnt = {}
        self.pending_bar = {}
        self.last_on_eng = {}
        self.last_dma = {}

    def add(self, eng, fn, r=(), w=(), dma=None):
        i = len(self.ops)
        deps = set()
        for k in r:
            if k in self.lastw:
                deps.add(self.lastw[k])
        for k in w:
            if k in self.lastw:
                deps.add(self.lastw[k])
            deps |= self.readers.get(k, set())
        if eng in self.pending_bar:
            deps |= self.pending_bar.pop(eng)
        deps.discard(i)
        op = dict(id=i, eng=eng, fn=fn, deps=deps, dma=dma, sig=False, cnt=0, seq=0)
        if dma is not None:
            self.dma_cnt[dma] = self.dma_cnt.get(dma, 0) + 1
            op["cnt"] = self.dma_cnt[dma]
            self.last_dma[dma] = i
        for k in r:
            self.readers.setdefault(k, set()).add(i)
        for k in w:
            self.lastw[k] = i
            self.readers[k] = set()
        self.ops.append(op)
        self.last_on_eng[eng] = i
        return i

    def barrier(self):
        allp = set(self.last_on_eng.values()) | set(self.last_dma.values())
        for e in ("pe", "act", "dve", "pool", "sp"):
            self.pending_bar[e] = set(allp) | self.pending_bar.get(e, set())

    def emit(self, nc, block, sems):
        ops = self.ops
        for op in ops:
            keep = set()
            for d in op["deps"]:
                y = ops[d]
                if y["dma"] is None and op["dma"] is None and y["eng"] == op["eng"]:
                    if op["eng"] == "pe" or not SAME_ENG_SYNC:
                        continue
                if y["dma"] is None and y["eng"] == op["eng"] and op["dma"] is not None and False:
                    continue
                keep.add(d)
            best = {}
            for d in keep:
                y = ops[d]
                key = ("d", y["dma"]) if y["dma"] is not None else ("e", y["eng"])
                if key not in best or best[key] < d:
                    best[key] = d
            keep = set(best.values())
            op["deps"] = keep
            for d in keep:
                if ops[d]["dma"] is None:
                    ops[d]["sig"] = True
        seqc = {}
        for op in ops:
            if op["dma"] is None and op["sig"]:
                seqc[op["eng"]] = seqc.get(op["eng"], 0) + 1
                op["seq"] = seqc[op["eng"]]
        dma_keys = sorted(self.dma_cnt.keys(), key=str)
        self.stats = dict(n_ops=len(ops), seq=dict(seqc), n_dma_keys=len(dma_keys))
        eng_sem = {e: sems[n] for n, e in enumerate(("pe", "act", "dve", "pool", "sp"))}
        dma_sem = {k: sems[5 + n] for n, k in enumerate(dma_keys)}
        assert 5 + len(dma_keys) <= len(sems), (len(dma_keys), len(sems))

        def run(engname, eng):
            waited = {}
            for op in ops:
                if op["eng"] != engname:
                    continue
                for d in sorted(op["deps"]):
                    y = ops[d]
                    if y["dma"] is not None:
                        s, v, key = dma_sem[y["dma"]], 16 * y["cnt"], ("d", y["dma"])
                    else:
                        s, v, key = eng_sem[y["eng"]], y["seq"], ("e", y["eng"])
                    if waited.get(key, 0) < v:
                        eng.wait_ge(s, v)
                        waited[key] = v
                if op["dma"] is not None and op["cnt"] > 1:
                    k2 = ("d", op["dma"])
                    v2 = 16 * (op["cnt"] - 1)
                    if waited.get(k2, 0) < v2:
                        eng.wait_ge(dma_sem[op["dma"]], v2)
                        waited[k2] = v2
                ins = op["fn"](eng)
                if op["dma"] is not None:
                    ins.then_inc(dma_sem[op["dma"]], 16)
                elif op["sig"]:
                    ins.then_inc(eng_sem[op["eng"]], 1)
            if engname == "sp":
                for k in dma_keys:
                    v = 16 * self.dma_cnt[k]
                    if waited.get(("d", k), 0) < v:
                        eng.wait_ge(dma_sem[k], v)

        @block.tensor
        def _(e):
            run("pe", e)

        @block.scalar
        def _(e):
            run("act", e)

        @block.vector
        def _(e):
            run("dve", e)

        @block.gpsimd
        def _(e):
            run("pool", e)

        @block.sync
        def _(e):
            run("sp", e)


def build_program():
    nc = bass.Bass("TRN2", target_bir_lowering=False)

    def din(name, shape, dt=F32):
        return nc.dram_tensor(name, list(shape), dt, kind="ExternalInput").ap()

    xo_d = din("xo", [T, D])
    xp_d = din("xp", [T, D])
    crep_d = din("crep", [128, D])
    wada_d = din("wada", [72, 128, 1024])
    wgu_d = [din("wgu1", [11, 128, 4096]), din("wgu2", [11, 128, 4096])]
    wd_d = [din("wd1", [8, 128, DFF]), din("wd2", [8, 128, DFF])]
    wmix_d = din("wmix", [10, 128, 4096])
    wco_d = din("wco", [128, 4096])
    wao_d = din("wao", [128, 4096])
    wo_d = din("wo", [2, 128, 4096])
    vecs_d = din("vecs", [128, 48])
    bada_d = din("bada", [128, 72])
    convw_d = din("convw", [128, 12])
    ident_d = din("ident", [128, 128])
    tri_d = din("tri", [128, 3, 128])
    mk_d = din("mk", [128, 2, 128])
    om_d = din("om", [128, 1])
    y_d = nc.dram_tensor("y", [T, D], F32, kind="ExternalOutput").ap()
    dbg_d = None
    if DEBUG_TAP is not None:
        dbg_d = nc.dram_tensor("dbg", [128, KC, T], F32, kind="ExternalOutput").ap()
    kd_d = nc.dram_tensor("kd", [4, 128, 32, 128], BF16, kind="Internal").ap()
    qd_d = nc.dram_tensor("qd", [4, 128, T], BF16, kind="Internal").ap()
    vd_d = nc.dram_tensor("vd", [32, 128, 512], BF16, kind="Internal").ap()

    import contextlib
    es = contextlib.ExitStack()

    def sb(name, shape, dt):
        return es.enter_context(nc.sbuf_tensor(name, list(shape), dt))

    with es:
        hT = sb("hT", [128, KC, T], F32)
        wbf = [sb(f"wbf{i}", [128, 4096], BF16) for i in range(3)]
        identF = sb("identF", [128, 128], F32)
        tri = sb("trib", [128, 3, 128], BF16)
        mk = sb("mkb", [128, 2, 128], BF16)
        om = sb("oms", [128, 1], F32)
        vecs = sb("vecss", [128, 48], F32)
        bada = sb("badas", [128, 72], F32)
        convw = sb("convws", [128, 12], F32)
        modT = sb("modT", [128, 72], F32)
        AB = sb("AB", [128, 9, KC], F32)
        rstd2 = sb("rstd2", [128, T], F32)
        halo = sb("halo", [128, 4, NBLK, 2], F32)
        fgv = sb("fgv", [128, KC], F32)
        R1 = sb("R1", [128, 108544], U8)
        psp = [es.enter_context(nc.psum_tensor(f"ps{i}", [128, 1024], F32)) for i in range(4)]
        psb = [psp[i // 2][:, (i % 2) * 512:(i % 2 + 1) * 512] for i in range(8)]
        sems = [es.enter_context(nc.semaphore(f"s{i}")) for i in range(60)]
        block = es.enter_context(nc.Block())

        P = Prog()
        off = [0]

        def carve(nbytes, dt, shape=None):
            a = R1[:, off[0]:off[0] + nbytes].bitcast(dt)
            off[0] += nbytes
            assert off[0] <= 108544, off[0]
            return a

        def r3(ap, a):
            return ap.rearrange("p (a b) -> p a b", a=a)

        off[0] = 0
        UT = r3(carve(16384, BF16), KC)
        ACTB = r3(carve(45056, BF16), NJ)
        XS = [carve(4096, F32) for _ in range(2)]
        SQ = r3(carve(8192, BF16), KC)
        LNV = carve(2048, F32)
        RSTD = carve(2048, F32)
        TMP = [carve(2048, F32) for _ in range(2)]
        SG = [carve(2048, F32) for _ in range(2)]
        QS = [carve(1024, BF16) for _ in range(2)]
        UT2 = r3(carve(16384, BF16), KC)
        assert off[0] <= 108544
        WAS = [R1[:, 16384 + i * 16384:16384 + (i + 1) * 16384].bitcast(F32) for i in range(2)]

        psrot = [0]

        def ps_next():
            b = psrot[0] % 8
            psrot[0] += 1
            return b

        wrot = [0]

        def wload(src_ap, n):
            s = wrot[0] % 3
            wrot[0] += 1
            dst = wbf[s][:, 0:n]
            P.add("pool", lambda e, dst=dst, src_ap=src_ap: e.dma_start(out=dst, in_=src_ap, max_dma_last_dim=4096),
                  w=[("wbf", s)], dma=("wbf", s))
            return s

        def cdma(dst, src, key, eng="sp"):
            P.add(eng, lambda e, dst=dst, src=src: e.dma_start(out=dst, in_=src), w=[key], dma="const")

        cdma(identF[:], ident_d, "c_ident")
        cdma(vecs[:], vecs_d, "c_vecs")
        cdma(bada[:], bada_d, "c_bada")
        cdma(convw[:], convw_d, "c_convw")
        cdma(om[:], om_d, "c_om")
        P.add("pool", lambda e: e.dma_start(out=tri[:], in_=tri_d), w=["c_tri"], dma="constp")
        P.add("pool", lambda e: e.dma_start(out=mk[:], in_=mk_d), w=["c_mk"], dma="constp")
        CONST = ["c_ident", "c_vecs", "c_bada", "c_convw", "c_tri", "c_mk", "c_om"]

        crep = rstd2[:, 0:1024]
        mscr = R1[:, 69632:73728].bitcast(F32)
        modrot = [0]

        MST = [R1[:, 77824:81920].bitcast(F32), R1[:, 81920:86016].bitcast(F32)]
        MSTK = [["LNV", "RSTD"], [("TMP", 0), ("TMP", 1)]]

        def mod_ops(oc_list, early=False):
            for oc in oc_list:
                s = modrot[0] % 2
                modrot[0] += 1
                if early:
                    stg, keys, dk = MST[s], MSTK[s], ("mst", s)
                else:
                    stg, keys, dk = XS[s], [("xs", s)], ("xs", s)
                P.add("sp", lambda e, stg=stg, oc=oc: e.dma_start(out=stg, in_=wada_d[oc]),
                      w=keys, dma=dk)
                P.add("dve", lambda e, stg=stg, oc=oc: e.scalar_tensor_tensor(
                    out=mscr, in0=stg, scalar=1.0, in1=crep, op0=ALU.mult, op1=ALU.mult,
                    accum_out=modT[:, oc:oc + 1]),
                    r=keys + ["crep"], w=[("modc", oc // 24), "SQ"])
                if oc % 24 == 23:
                    sl = oc // 24
                    base = sl * 24
                    P.add("dve", lambda e, base=base: e.tensor_tensor(
                        out=modT[:, base:base + 24], in0=modT[:, base:base + 24], in1=bada[:, base:base + 24],
                        op=ALU.add), r=[("modc", sl)] + CONST, w=[("modc", sl)])
                    gv = vecs[:, sl * 8:(sl + 1) * 8]
                    P.add("dve", lambda e, sl=sl, base=base, gv=gv: e.scalar_tensor_tensor(
                        out=AB[:, sl * 3 + 0, :], in0=modT[:, base + 8:base + 16], scalar=1.0, in1=gv,
                        op0=ALU.add, op1=ALU.mult), r=[("modc", sl)], w=[("AB", sl, 0)])
                    P.add("dve", lambda e, sl=sl, base=base: e.tensor_copy(
                        out=AB[:, sl * 3 + 1, :], in_=modT[:, base:base + 8]), r=[("modc", sl)], w=[("AB", sl, 1)])
                    fac = 1.0 if sl == 1 else 0.5
                    P.add("dve", lambda e, sl=sl, base=base, fac=fac: e.tensor_scalar(
                        out=AB[:, sl * 3 + 2, :], in0=modT[:, base + 16:base + 24], scalar1=fac, scalar2=None,
                        op0=ALU.mult), r=[("modc", sl)], w=[("AB", sl, 2)])

        def phase0_mod():
            cdma(crep, crep_d, "c_crep")
            P.add("act", lambda e: e.activation(out=crep, in_=crep, func=AF.Silu), r=["c_crep"], w=["crep"])
            P.add("dve", lambda e: e.tensor_copy(out=fgv[:], in_=vecs[:, 24:32]), r=CONST, w=["fgv"])
            mod_ops(range(24), early=True)

        mod_rest = list(range(24, 72))

        def load_x(x_ap, Hk, Hv, ntok_blocks, queue="sp", blocks=None):
            for b in (range(ntok_blocks) if blocks is None else blocks):
                s = b % 2
                P.add(queue, lambda e, s=s, b=b: e.dma_start(out=XS[s], in_=x_ap[b * 128:(b + 1) * 128, :]),
                      w=[("xs", s)], dma=("xs", s))
                for half in range(2):
                    pb = ps_next()
                    for q in range(4):
                        kc = half * 4 + q
                        P.add("pe", lambda e, s=s, kc=kc, q=q, pb=pb: e.transpose(
                            psb[pb][:, q * 128:(q + 1) * 128], XS[s][:, kc * 128:(kc + 1) * 128], identF[:]),
                            r=[("xs", s)] + CONST, w=[("ps", pb)])
                    src = psb[pb].rearrange("p (a b) -> p a b", a=4)
                    dst = Hv[:, half * 4:half * 4 + 4, b * 128:(b + 1) * 128]
                    if half == 0:
                        P.add("act", lambda e, src=src, dst=dst: e.copy(out=dst, in_=src),
                              r=[("ps", pb)], w=[Hk])
                    else:
                        P.add("dve", lambda e, src=src, dst=dst: e.tensor_copy(out=dst, in_=src),
                              r=[("ps", pb)], w=[Hk])

        def norm_mod(Hk, Hv, ntiles, sl, Uk, Uv, keep_rstd=None, tile0=0, tiles=None, part=None):
            for tt in (range(ntiles) if tiles is None else tiles):
                cs = slice(tt * 512, (tt + 1) * 512)
                if part in (None, "sq"):
                    P.add("act", lambda e, cs=cs: e.activation(out=SQ[:], in_=Hv[:, :, cs], func=AF.Square),
                          r=[Hk], w=["SQ"])
                if part == "sq":
                    continue
                pb = ps_next()
                for kc in range(KC):
                    P.add("pe", lambda e, kc=kc, pb=pb: e.matmul(
                        psb[pb], lhsT=tri[:, 2, :], rhs=SQ[:, kc, :], start=(kc == 0), stop=(kc == KC - 1)),
                        r=["SQ"] + CONST, w=[("ps", pb)])
                P.add("act", lambda e, pb=pb: e.activation(out=LNV, in_=psb[pb], func=AF.Ln,
                                                          bias=EPS, scale=1.0 / D),
                      r=[("ps", pb)], w=["LNV"])
                if keep_rstd is not None:
                    rs = keep_rstd[:, (tile0 + tt) * 512:(tile0 + tt + 1) * 512]
                    rk = ("rstd2", tile0 + tt)
                    P.add("act", lambda e: e.copy(out=LNV[:, 0:2], in_=LNV[:, 0:2]), r=["LNV"], w=["LNV", "crep"])
                else:
                    rs = RSTD
                    rk = "RSTD"
                P.add("act", lambda e, rs=rs: e.activation(out=rs, in_=LNV, func=AF.Exp, scale=-0.5),
                      r=["LNV"], w=[rk])
                modulate(Hk, Hv, cs, rs, rk, sl, Uk, Uv, cs)

        def modulate(Hk, Hv, hcs, rs, rk, sl, Uk, Uv, ucs):
            for kc in range(KC):
                s = kc % 2
                P.add("dve", lambda e, kc=kc, s=s: e.tensor_tensor(
                    out=TMP[s], in0=Hv[:, kc, hcs], in1=rs, op=ALU.mult),
                    r=[Hk, rk], w=[("TMP", s)])
                P.add("act", lambda e, kc=kc, s=s: e.activation(
                    out=Uv[:, kc, ucs], in_=TMP[s], func=AF.Identity,
                    scale=AB[:, sl * 3 + 0, kc:kc + 1], bias=AB[:, sl * 3 + 1, kc:kc + 1]),
                    r=[("TMP", s), ("AB", sl, 0), ("AB", sl, 1)], w=[Uk])

        def ffn(Hk, Hv, sl, f, hooksA=None, hooksB=None):
            hooksA = hooksA or {}
            hooksB = hooksB or {}
            for fn in hooksA.get(-1, []):
                fn()
            for jp in range(11):
                s = wload(wgu_d[f][jp], 4096)
                wv = wbf[s][:].rearrange("p (j k m) -> p j k m", j=2, k=KC)
                for jj in range(2):
                    j = jp * 2 + jj
                    for tt in range(2):
                        cs = slice(tt * 512, (tt + 1) * 512)
                        pg = ps_next()
                        pu = ps_next()
                        for kc in range(KC):
                            P.add("pe", lambda e, wv=wv, jj=jj, kc=kc, pg=pg, cs=cs: e.matmul(
                                psb[pg], lhsT=wv[:, jj, kc, 0:128], rhs=UT[:, kc, cs],
                                start=(kc == 0), stop=(kc == KC - 1)),
                                r=[("wbf", s), "UT"], w=[("ps", pg)])
                        for kc in range(KC):
                            P.add("pe", lambda e, wv=wv, jj=jj, kc=kc, pu=pu, cs=cs: e.matmul(
                                psb[pu], lhsT=wv[:, jj, kc, 128:256], rhs=UT[:, kc, cs],
                                start=(kc == 0), stop=(kc == KC - 1)),
                                r=[("wbf", s), "UT"], w=[("ps", pu)])
                        sg = (j * 2 + tt) % 2
                        P.add("act", lambda e, pg=pg, sg=sg: e.activation(out=SG[sg], in_=psb[pg], func=AF.Silu),
                              r=[("ps", pg)], w=[("SG", sg)])
                        P.add("dve", lambda e, pu=pu, sg=sg, j=j, cs=cs: e.tensor_tensor(
                            out=ACTB[:, j, cs], in0=SG[sg], in1=psb[pu], op=ALU.mult),
                            r=[("ps", pu), ("SG", sg)], w=[("ACTB", j, tt)])
                if mod_rest:
                    take, mod_rest[:] = mod_rest[:5], mod_rest[5:]
                    mod_ops(take)
                for fn in hooksA.get(jp, []):
                    fn()
            for dc in range(KC):
                s = wload(wd_d[f][dc], DFF)
                wv = wbf[s][:, 0:DFF].rearrange("p (j m) -> p j m", j=NJ)
                for tt in range(2):
                    cs = slice(tt * 512, (tt + 1) * 512)
                    py = ps_next()
                    for jc in range(NJ):
                        P.add("pe", lambda e, wv=wv, jc=jc, py=py, cs=cs: e.matmul(
                            psb[py], lhsT=wv[:, jc, :], rhs=ACTB[:, jc, cs],
                            start=(jc == 0), stop=(jc == NJ - 1)),
                            r=[("wbf", s), ("ACTB", jc, tt)], w=[("ps", py)])
                    P.add("dve", lambda e, dc=dc, py=py, cs=cs: e.scalar_tensor_tensor(
                        out=Hv[:, dc, cs], in0=psb[py], scalar=AB[:, sl * 3 + 2, dc:dc + 1],
                        in1=Hv[:, dc, cs], op0=ALU.mult, op1=ALU.add),
                        r=[("ps", py), ("AB", sl, 2)], w=[Hk])
                for fn in hooksB.get(dc, []):
                    fn()

        def proj_T(widx, n_chunks_lo, Uk, dst_fn, scale, tag):
            s = wload(wmix_d[widx], 4096)
            wv = wbf[s][:].rearrange("p (k n) -> p k n", k=KC)
            for c in range(4):
                for tt in range(2):
                    cs = slice(tt * 512, (tt + 1) * 512)
                    pb = ps_next()
                    for kc in range(KC):
                        P.add("pe", lambda e, wv=wv, c=c, kc=kc, pb=pb, cs=cs: e.matmul(
                            psb[pb], lhsT=wv[:, kc, c * 128:(c + 1) * 128], rhs=UT2[:, kc, cs],
                            start=(kc == 0), stop=(kc == KC - 1)),
                            r=[("wbf", s), Uk], w=[("ps", pb)])
                    dst_fn(c, tt, pb)

        qsrot = [0]

        def st_info(sidx):
            is_own = sidx >= 2
            st = sidx % 2
            Hk = ("H", st)
            Hv = hT[:, :, st * 1024:(st + 1) * 1024]
            x_ap = (xo_d if is_own else xp_d)[st * 1024:(st + 1) * 1024, :]
            return is_own, st, Hk, Hv, x_ap

        def pre_load(sidx, queue="sp", blocks=None):
            is_own, st, Hk, Hv, x_ap = st_info(sidx)
            load_x(x_ap, Hk, Hv, 8, queue, blocks)

        def pre_norm(sidx, tt, part=None):
            is_own, st, Hk, Hv, x_ap = st_info(sidx)
            norm_mod(Hk, Hv, 2, 0, "UT", UT, tiles=[tt], part=part)

        def n2(sidx, tt, part=None):
            is_own, st, Hk, Hv, x_ap = st_info(sidx)
            norm_mod(Hk, Hv, 2, 1, "UT2", UT2, keep_rstd=(rstd2 if is_own else None), tile0=st * 2, tiles=[tt],
                     part=part)

        def qkv(sidx):
            is_own, st, Hk, Hv, x_ap = st_info(sidx)
            par = 1 if is_own else 0

            def evac_to_dram(dram_fn, scale):
                def f(c, tt, pb):
                    q = qsrot[0] % 2
                    qsrot[0] += 1
                    P.add("act", lambda e, pb=pb, q=q: e.activation(
                        out=QS[q], in_=psb[pb], func=AF.Identity, scale=scale),
                        r=[("ps", pb)], w=[("QS", q)])
                    dst = dram_fn(c, tt)
                    src = QS[q] if len(dst.shape) == 2 else QS[q].rearrange("p (a b) -> p a b", a=4)
                    P.add("sp", lambda e, dst=dst, src=src: e.dma_start(out=dst, in_=src),
                          r=[("QS", q)], dma=("qs", q))
                return f

            if is_own:
                proj_T(3, 4, "UT2", evac_to_dram(
                    lambda c, tt: qd_d[c][:, st * 1024 + tt * 512: st * 1024 + (tt + 1) * 512], 0.125), 0.125, "q")
            i0f = lambda tt: st * 8 + tt * 4
            proj_T(4, 4, "UT2", evac_to_dram(
                lambda c, tt: kd_d[c][:, 2 * i0f(tt) + par: 2 * i0f(tt) + par + 7: 2, :], 1.0), 1.0, "k")
            s = wload(wmix_d[5], 4096)
            wv = wbf[s][:].rearrange("p (k n) -> p k n", k=KC)
            for b in range(8):
                pb = ps_next()
                for kc in range(KC):
                    P.add("pe", lambda e, wv=wv, kc=kc, pb=pb, b=b: e.matmul(
                        psb[pb], lhsT=UT2[:, kc, b * 128:(b + 1) * 128], rhs=wv[:, kc, :],
                        start=(kc == 0), stop=(kc == KC - 1)),
                        r=[("wbf", s), "UT2"], w=[("ps", pb)])
                q = qsrot[0] % 2
                qsrot[0] += 1
                P.add("act", lambda e, pb=pb, q=q: e.copy(out=QS[q], in_=psb[pb]),
                      r=[("ps", pb)], w=[("QS", q)])
                sig = 2 * (st * 8 + b) + par
                P.add("sp", lambda e, q=q, sig=sig: e.dma_start(out=vd_d[sig], in_=QS[q]),
                      r=[("QS", q)], dma=("qs", q))
            if not is_own:
                ucols = UT2[:, :, :].rearrange("p k (b t) -> p k b t", t=128)
                pcs = []
                for widx in (1, 2):
                    s = wload(wmix_d[widx], 4096)
                    wv = wbf[s][:].rearrange("p (k n) -> p k n", k=KC)
                    pb = ps_next()
                    for c in range(4):
                        for kc in range(KC):
                            P.add("pe", lambda e, wv=wv, c=c, kc=kc, pb=pb: e.matmul(
                                psb[pb][:, c * 16:(c + 1) * 16].rearrange("p (b t) -> p b t", t=2),
                                lhsT=wv[:, kc, c * 128:(c + 1) * 128], rhs=ucols[:, kc, :, 126:128],
                                start=(kc == 0), stop=(kc == KC - 1)),
                                r=[("wbf", s), "UT2"], w=[("ps", pb)])
                    pcs.append(pb)
                P.add("act", lambda e, pb=pcs[0]: e.copy(out=TMP[0][:, 0:64], in_=psb[pb][:, 0:64]),
                      r=[("ps", pcs[0])], w=[("TMP", 0)])
                P.add("dve", lambda e, pb=pcs[1], st=st: e.tensor_tensor(
                    out=halo[:, :, st * 8:(st + 1) * 8, :],
                    in0=TMP[0][:, 0:64].rearrange("p (c b t) -> p c b t", c=4, t=2),
                    in1=psb[pb][:, 0:64].rearrange("p (c b t) -> p c b t", c=4, t=2), op=ALU.mult),
                    r=[("ps", pcs[1]), ("TMP", 0)], w=["halo"])
                if st == 0:
                    P.add("dve", lambda e: e.tensor_scalar(
                        out=halo[:, :, 0, :], in0=halo[:, :, 0, :], scalar1=om[:, 0:1], scalar2=None,
                        op0=ALU.mult), r=["halo"] + CONST, w=["halo"])

        phase0_mod()
        pre_load(0, "pool")
        pre_norm(0, 0)
        pre_norm(0, 1)
        for sidx in range(4):
            hA, hB = {}, {}
            if sidx > 0:
                hA = {-1: [lambda p=sidx - 1: n2(p, 0, "sq")],
                      0: [lambda p=sidx - 1: n2(p, 0, "rest")],
                      1: [lambda p=sidx - 1: n2(p, 1, "sq")],
                      2: [lambda p=sidx - 1: n2(p, 1, "rest")],
                      4: [lambda p=sidx - 1: qkv(p)]}
            if sidx < 3:
                hB = {0: [lambda n=sidx + 1: pre_load(n, "sp", [0, 1])],
                      1: [lambda n=sidx + 1: pre_load(n, "sp", [2, 3])],
                      2: [lambda n=sidx + 1: pre_load(n, "sp", [4, 5]), lambda n=sidx + 1: pre_norm(n, 0, "sq")],
                      3: [lambda n=sidx + 1: pre_load(n, "sp", [6, 7]), lambda n=sidx + 1: pre_norm(n, 0, "rest")],
                      4: [lambda n=sidx + 1: pre_norm(n, 1, "sq")],
                      5: [lambda n=sidx + 1: pre_norm(n, 1, "rest")]}
            is_own, st, Hk, Hv, x_ap = st_info(sidx)
            ffn(Hk, Hv, 0, 0, hA, hB)
        n2(3, 0)
        n2(3, 1)
        qkv(3)

        if DEBUG_TAP == "h1":
            P.add("sp", lambda e: e.dma_start(out=dbg_d, in_=hT[:]), r=[("H", 0), ("H", 1)], w=["dbg"], dma="dbg")

        P.barrier()
        off[0] = 0
        OT = r3(carve(16384, BF16), 4)
        kt = carve(8192, BF16).rearrange("p (s t) -> p s t", s=32)
        ktn = carve(8192, BF16).rearrange("p (s t) -> p s t", s=32)
        QM = carve(8192, BF16).rearrange("p (h t) -> p h t", h=2)
        VM = carve(16384, BF16).rearrange("p (h s d) -> p h s d", h=2, s=32)
        EB2 = [carve(4096, F32) for _ in range(2)]
        SP2 = [carve(2048, BF16) for _ in range(3)]
        AH2 = [carve(2048, BF16) for _ in range(2)]

        kvk = "kv"
        P.add("dve", lambda e: e.memset(QM[64:128, 0, :], 0.0), w=[(kvk, "q")])
        P.add("dve", lambda e: e.memset(QM[0:64, 1, :], 0.0), w=[(kvk, "q")])
        P.add("dve", lambda e: e.memset(VM[:, 0, :, 64:128], 0.0), w=[(kvk, "v")])
        P.add("dve", lambda e: e.memset(VM[:, 1, :, 0:64], 0.0), w=[(kvk, "v")])
        for hp in range(4):
            P.add("sp", lambda e, hp=hp: e.dma_start(out=kt, in_=kd_d[hp]),
                  w=[(kvk, "k")], dma=(kvk, "k"))
            P.add("sp", lambda e, hp=hp: e.dma_start(out=QM[0:64, 0, :], in_=qd_d[hp][0:64, :]),
                  w=[(kvk, "q")], dma=(kvk, "q"))
            P.add("sp", lambda e, hp=hp: e.dma_start(out=QM[64:128, 1, :], in_=qd_d[hp][64:128, :]),
                  w=[(kvk, "q")], dma=(kvk, "q"))
            for h in range(2):
                P.add("sp", lambda e, hp=hp, h=h: e.dma_start(
                    out=VM[:, h, :, 64 * h:64 * h + 64],
                    in_=vd_d[:, :, hp * 128 + 64 * h:hp * 128 + 64 * h + 64].rearrange("s p d -> p s d")),
                    w=[(kvk, "v")], dma=(kvk, "v"))
            P.add("dve", lambda e: e.tensor_scalar(
                out=ktn, in0=kt, scalar1=-1.0, scalar2=None, op0=ALU.mult),
                r=[(kvk, "k")], w=[(kvk, "kn")])
            for qtile in range(4):
                nkb = 8 * qtile + 8
                po = 6 + (qtile % 2)
                cs = slice(qtile * 512, (qtile + 1) * 512)

                qs = qtile * 512

                def c0_of(m):
                    return max(0, 3 - m // 2) * 128 if m < 8 else 0

                def v3(ap):
                    return ap.rearrange("p (h t) -> p h t", h=2)

                def masks(m, target_fn, key):
                    sig = nkb - 1 - m
                    c0 = c0_of(m)
                    if m < 8 and m % 2 == 0:
                        for h in range(2):
                            tg = target_fn(h)[:, c0:c0 + 128]
                            P.add("dve", lambda e, tg=tg: e.tensor_tensor(out=tg, in0=tg, in1=mk[:, 0, :], op=ALU.mult),
                                  r=[key] + CONST, w=[key])
                    if sig == 0:
                        for h in range(2):
                            tg = target_fn(h)[:, c0:512]
                            P.add("dve", lambda e, tg=tg: e.tensor_scalar(
                                out=tg, in0=tg, scalar1=om[:, 0:1], scalar2=None, op0=ALU.mult),
                                r=[key] + CONST, w=[key])

                def Zp(m):
                    sig = nkb - 1 - m
                    par = m % 2
                    c0 = c0_of(m)
                    for h in range(2):
                        pz = par * 2 + h
                        P.add("pe", lambda e, pz=pz, sig=sig, h=h, c0=c0, qs=qs: e.matmul(
                            psb[pz][:, c0:512], lhsT=kt[:, sig, :], rhs=QM[:, h, qs + c0:qs + 512],
                            start=True, stop=True),
                            r=[(kvk, "k"), (kvk, "q")], w=[("zp", par)])

                def S1a(m):
                    par = m % 2
                    sp3 = m % 3
                    c0 = c0_of(m)
                    P.add("act", lambda e, par=par, c0=c0: e.activation(
                        out=v3(EB2[par])[:, :, c0:512], in_=v3(psp[par][:])[:, :, c0:512], func=AF.Exp),
                        r=[("zp", par)], w=[("EB2", par)])
                    masks(m, lambda h, par=par: EB2[par][:, h * 512:(h + 1) * 512], ("EB2", par))
                    P.add("act", lambda e, par=par, sp3=sp3, c0=c0: e.activation(
                        out=v3(SP2[sp3])[:, :, c0:512], in_=v3(EB2[par])[:, :, c0:512], func=AF.Ln,
                        bias=1.0, scale=1.0),
                        r=[("EB2", par)], w=[("SP2", sp3)])

                def BZL(m):
                    sig = nkb - 1 - m
                    sp3 = m % 3
                    c0 = c0_of(m)
                    for h in range(2):
                        pbk = 4 + h
                        sp = SP2[sp3][:, h * 512 + c0:(h + 1) * 512]
                        P.add("pe", lambda e, h=h, pbk=pbk, sig=sig, m=m, c0=c0, qs=qs: e.matmul(
                            psb[pbk][:, c0:512], lhsT=kt[:, sig, :], rhs=QM[:, h, qs + c0:qs + 512],
                            start=(m == 0), stop=False, skip_group_check=True),
                            r=[(kvk, "k"), (kvk, "q")], w=["BB"])
                        P.add("pe", lambda e, pbk=pbk, sp=sp, c0=c0: e.matmul(
                            psb[pbk][:, c0:512], lhsT=tri[:, 0, :], rhs=sp, start=False, stop=False,
                            skip_group_check=True),
                            r=[("SP2", sp3)] + CONST, w=["BB"])

                def A2(m):
                    par = m % 2
                    c0 = c0_of(m)
                    P.add("act", lambda e, par=par, c0=c0: e.activation(
                        out=v3(AH2[par])[:, :, c0:512], in_=v3(psp[2][:])[:, :, c0:512], func=AF.Exp),
                        r=["BB"], w=[("AH2", par)])
                    masks(m, lambda h, par=par: AH2[par][:, h * 512:(h + 1) * 512], ("AH2", par))

                def UZ(m):
                    sig = nkb - 1 - m
                    sp3 = m % 3
                    c0 = c0_of(m)
                    for h in range(2):
                        pbk = 4 + h
                        sp = SP2[sp3][:, h * 512 + c0:(h + 1) * 512]
                        P.add("pe", lambda e, pbk=pbk, sp=sp, c0=c0: e.matmul(
                            psb[pbk][:, c0:512], lhsT=tri[:, 1, :], rhs=sp, start=False, stop=False,
                            skip_group_check=True),
                            r=[("SP2", sp3)] + CONST, w=["BB"])
                        P.add("pe", lambda e, h=h, pbk=pbk, sig=sig, m=m, c0=c0, qs=qs, nkb=nkb: e.matmul(
                            psb[pbk][:, c0:512], lhsT=ktn[:, sig, :], rhs=QM[:, h, qs + c0:qs + 512],
                            start=False, stop=(m == nkb - 1), skip_group_check=True),
                            r=[(kvk, "kn"), (kvk, "q")], w=["BB"])

                def AV(m):
                    sig = nkb - 1 - m
                    par = m % 2
                    c0 = c0_of(m)
                    for h in range(2):
                        ah = AH2[par][:, h * 512 + c0:(h + 1) * 512]
                        P.add("pe", lambda e, h=h, sig=sig, ah=ah, m=m, po=po, nkb=nkb, c0=c0: e.matmul(
                            psb[po][:, c0:512], lhsT=VM[:, h, sig, :], rhs=ah, start=(m == 0 and h == 0),
                            stop=(m == nkb - 1 and h == 1), skip_group_check=True),
                            r=[(kvk, "v"), ("AH2", par)], w=[("ps", po)])

                def S2(m, h):
                    sig = nkb - 1 - m
                    par = m % 2
                    c0 = c0_of(m)
                    pbk = 4 + h
                    Bk = ("ps", pbk)
                    sp = SP2[par][:, h * 512 + c0:(h + 1) * 512]
                    P.add("pe", lambda e, h=h, pbk=pbk, sig=sig, m=m, c0=c0, qs=qs: e.matmul(
                        psb[pbk][:, c0:512], lhsT=kt[:, sig, :], rhs=QM[:, h, qs + c0:qs + 512],
                        start=(m == 0), stop=False, skip_group_check=True),
                        r=[(kvk, "k"), (kvk, "q")], w=[Bk])
                    P.add("pe", lambda e, pbk=pbk, sp=sp, c0=c0: e.matmul(
                        psb[pbk][:, c0:512], lhsT=tri[:, 0, :], rhs=sp, start=False, stop=False,
                        skip_group_check=True),
                        r=[("SP2", par)] + CONST, w=[Bk])
                    ah = AH[h][par]
                    P.add("act", lambda e, pbk=pbk, ah=ah, c0=c0: e.activation(
                        out=ah[:, c0:512], in_=psb[pbk][:, c0:512], func=AF.Exp),
                        r=[Bk], w=[("AH", h, par)])

                def S2m(m):
                    par = m % 2
                    for h in range(2):
                        masks_h(m, h, AH[h][par], ("AH", h, par))

                def masks_h(m, h, target, key):
                    sig = nkb - 1 - m
                    c0 = c0_of(m)
                    if m < 8 and m % 2 == 0:
                        tg = target[:, c0:c0 + 128]
                        P.add("dve", lambda e, tg=tg: e.tensor_tensor(out=tg, in0=tg, in1=mk[:, 0, :], op=ALU.mult),
                              r=[key] + CONST, w=[key])
                    if sig == 0:
                        tg = target[:, c0:512]
                        P.add("dve", lambda e, tg=tg: e.tensor_scalar(
                            out=tg, in0=tg, scalar1=om[:, 0:1], scalar2=None, op0=ALU.mult),
                            r=[key] + CONST, w=[key])

                def S3(m, h):
                    sig = nkb - 1 - m
                    par = m % 2
                    c0 = c0_of(m)
                    pbk = 4 + h
                    Bk = ("ps", pbk)
                    sp = SP2[par][:, h * 512 + c0:(h + 1) * 512]
                    ah = AH[h][par]
                    P.add("pe", lambda e, pbk=pbk, sp=sp, c0=c0: e.matmul(
                        psb[pbk][:, c0:512], lhsT=tri[:, 1, :], rhs=sp, start=False, stop=False,
                        skip_group_check=True),
                        r=[("SP2", par)] + CONST, w=[Bk])
                    P.add("pe", lambda e, h=h, pbk=pbk, sig=sig, m=m, c0=c0, qs=qs, nkb=nkb: e.matmul(
                        psb[pbk][:, c0:512], lhsT=ktn[:, sig, :], rhs=QM[:, h, qs + c0:qs + 512],
                        start=False, stop=(m == nkb - 1), skip_group_check=True),
                        r=[(kvk, "kn"), (kvk, "q")], w=[Bk])
                    P.add("pe", lambda e, h=h, sig=sig, ah=ah, m=m, po=po, nkb=nkb, c0=c0: e.matmul(
                        psb[po][:, c0:512], lhsT=VM[:, h, sig, :], rhs=ah[:, c0:512], start=(m == 0 and h == 0),
                        stop=(m == nkb - 1 and h == 1), skip_group_check=True),
                        r=[(kvk, "v"), ("AH", h, par)], w=[("ps", po)])

                Zp(0)
                Zp(1)
                S1a(0)
                Zp(2)
                S1a(1)
                BZL(0)
                for m in range(nkb):
                    A2(m)
                    UZ(m)
                    if m + 1 < nkb:
                        BZL(m + 1)
                    AV(m)
                    if m + 3 < nkb:
                        Zp(m + 3)
                    if m + 2 < nkb:
                        S1a(m + 2)
                P.add("dve", lambda e, po=po, hp=hp, cs=cs: e.tensor_copy(out=OT[:, hp, cs], in_=psb[po]),
                      r=[("ps", po)], w=[("OT", hp)])

        P.barrier()
        off[0] = 16384
        U2B = [r3(carve(8192, BF16), KC) for _ in range(2)]
        VC = carve(8320, F32).rearrange("p (c b t) -> p c b t", c=4, b=4)
        CV = r3(carve(8192, F32), 4)
        YAI = r3(carve(4096, BF16), 4)
        SGA = r3(carve(16384, F32), KC)
        SGB = r3(carve(16384, F32), KC)
        M1 = SGA
        MG = r3(carve(8192, BF16), KC)
        TM2 = [carve(2048, F32) for _ in range(2)]
        TM3 = [carve(2048, F32) for _ in range(2)]

        def wview(s):
            return wbf[s][:].rearrange("p (k n) -> p k n", k=KC)

        def u2_recompute(tt):
            gcs = slice(tt * 512, (tt + 1) * 512)
            Hk = ("H", tt // 2)
            U2 = U2B[tt % 2]
            for kc in range(KC):
                s2 = kc % 2
                P.add("dve", lambda e, kc=kc, s2=s2, gcs=gcs: e.tensor_tensor(
                    out=TM2[s2], in0=hT[:, kc, gcs], in1=rstd2[:, gcs], op=ALU.mult),
                    r=[Hk, ("rstd2", tt)], w=[("TM2", s2)])
                P.add("act", lambda e, kc=kc, s2=s2, U2=U2: e.activation(
                    out=U2[:, kc, :], in_=TM2[s2], func=AF.Identity,
                    scale=AB[:, 3, kc:kc + 1], bias=AB[:, 4, kc:kc + 1]),
                    r=[("TM2", s2), ("AB", 1, 0), ("AB", 1, 1)], w=[("U2", tt % 2)])

        u2_recompute(0)
        for tt in range(4):
            gcs = slice(tt * 512, (tt + 1) * 512)
            Hk = ("H", tt // 2)
            U2 = U2B[tt % 2]
            U2k = ("U2", tt % 2)
            P.add("dve", lambda e, tt=tt: e.tensor_copy(out=VC[:, :, :, 0:2], in_=halo[:, :, tt * 4:(tt + 1) * 4, :]),
                  r=["halo"], w=["VC"])

            def proj_tile(widx, fn, U2=U2, U2k=U2k):
                s = wload(wmix_d[widx], 4096)
                wv = wview(s)
                for c in range(4):
                    pb = ps_next()
                    for kc in range(KC):
                        P.add("pe", lambda e, wv=wv, c=c, kc=kc, pb=pb, U2=U2: e.matmul(
                            psb[pb], lhsT=wv[:, kc, c * 128:(c + 1) * 128], rhs=U2[:, kc, :],
                            start=(kc == 0), stop=(kc == KC - 1)),
                            r=[("wbf", s), U2k], w=[("ps", pb)])
                    fn(c, pb)

            def f_cc(c, pb):
                P.add("act", lambda e, c=c, pb=pb: e.copy(out=CV[:, c, :], in_=psb[pb]),
                      r=[("ps", pb)], w=[("CV", c)])
            proj_tile(1, f_cc)

            def f_cx(c, pb):
                P.add("dve", lambda e, c=c, pb=pb: e.tensor_tensor(
                    out=VC[:, c, :, 2:130], in0=CV[:, c, :].rearrange("p (b t) -> p b t", b=4),
                    in1=psb[pb].rearrange("p (b t) -> p b t", b=4), op=ALU.mult),
                    r=[("ps", pb), ("CV", c)], w=["VC"])
            proj_tile(2, f_cx)
            for c in range(4):
                cvv = CV[:, c, :].rearrange("p (b t) -> p b t", b=4)
                P.add("dve", lambda e, c=c, cvv=cvv: e.tensor_scalar(
                    out=cvv, in0=VC[:, c, :, 0:128], scalar1=convw[:, c * 3:c * 3 + 1], scalar2=None, op0=ALU.mult),
                    r=["VC", ("CV", c)] + CONST, w=[("CV", c)])
                for k in (1, 2):
                    P.add("dve", lambda e, c=c, k=k, cvv=cvv: e.scalar_tensor_tensor(
                        out=cvv, in0=VC[:, c, :, k:k + 128], scalar=convw[:, c * 3 + k:c * 3 + k + 1],
                        in1=cvv, op0=ALU.mult, op1=ALU.add),
                        r=["VC", ("CV", c)] + CONST, w=[("CV", c)])

            def gates(gidx0, SGX, bm_off, tag):
                for gp in range(2):
                    def f_g(c, pb, gp=gp):
                        oc = gp * 4 + c
                        P.add("act", lambda e, pb=pb, oc=oc: e.activation(
                            out=SGX[:, oc, :], in_=psb[pb], func=AF.Sigmoid,
                            bias=vecs[:, bm_off + oc:bm_off + oc + 1], scale=1.0),
                            r=[("ps", pb)] + CONST, w=[(tag, oc)])
                    proj_tile(gidx0 + gp, f_g)
            gates(6, SGA, 32, "SGA")
            gates(8, SGB, 40, "SGB")
            if tt + 1 < 4:
                u2_recompute(tt + 1)

            def f_cb(c, pb):
                P.add("dve", lambda e, c=c, pb=pb: e.tensor_tensor(
                    out=YAI[:, c, :], in0=CV[:, c, :], in1=psb[pb], op=ALU.mult),
                    r=[("ps", pb), ("CV", c)], w=[("YAI", c)])
            proj_tile(0, f_cb)

            def branch(wsrc, rhs_fn, rhs_keys, first):
                sw = wload(wsrc, 4096)
                wvo = wbf[sw][:].rearrange("p (o c m) -> p o c m", o=KC, c=4)
                for oc in range(KC):
                    py = ps_next()
                    for cc in range(4):
                        P.add("pe", lambda e, wvo=wvo, oc=oc, cc=cc, py=py: e.matmul(
                            psb[py], lhsT=wvo[:, oc, cc, :], rhs=rhs_fn(cc),
                            start=(cc == 0), stop=(cc == 3)),
                            r=[("wbf", sw)] + rhs_keys, w=[("ps", py)])
                    if first:
                        P.add("dve", lambda e, py=py, oc=oc: e.tensor_tensor(
                            out=M1[:, oc, :], in0=SGA[:, oc, :], in1=psb[py], op=ALU.mult),
                            r=[("ps", py), ("SGA", oc)], w=[("SGA", oc)])
                    else:
                        s2 = oc % 2
                        P.add("dve", lambda e, py=py, s2=s2, oc=oc: e.tensor_tensor(
                            out=TM3[s2], in0=SGB[:, oc, :], in1=psb[py], op=ALU.mult),
                            r=[("ps", py), ("SGB", oc)], w=[("TM3", s2)])
                        P.add("dve", lambda e, s2=s2, oc=oc: e.tensor_tensor(
                            out=MG[:, oc, :], in0=TM3[s2], in1=M1[:, oc, :], op=ALU.add),
                            r=[("TM3", s2), ("SGA", oc)], w=[("MG", oc)])

            branch(wco_d, lambda cc: YAI[:, cc, :], [("YAI", c) for c in range(4)], True)
            branch(wao_d, lambda cc, gcs=gcs: OT[:, cc, gcs], [("OT", c) for c in range(4)], False)
            for pc in range(2):
                s = wload(wo_d[pc], 4096)
                wv = wbf[s][:].rearrange("p (o k m) -> p o k m", o=4, k=KC)
                for o in range(4):
                    oc = pc * 4 + o
                    pb = ps_next()
                    for kc in range(KC):
                        P.add("pe", lambda e, wv=wv, o=o, kc=kc, pb=pb: e.matmul(
                            psb[pb], lhsT=wv[:, o, kc, :], rhs=MG[:, kc, :],
                            start=(kc == 0), stop=(kc == KC - 1)),
                            r=[("wbf", s)] + [("MG", k) for k in range(KC)], w=[("ps", pb)])
                    P.add("dve", lambda e, oc=oc, pb=pb, gcs=gcs: e.scalar_tensor_tensor(
                        out=hT[:, oc, gcs], in0=psb[pb], scalar=AB[:, 5, oc:oc + 1],
                        in1=hT[:, oc, gcs], op0=ALU.mult, op1=ALU.add),
                        r=[("ps", pb), ("AB", 1, 2)], w=[Hk])

        if DEBUG_TAP == "h2":
            P.add("sp", lambda e: e.dma_start(out=dbg_d, in_=hT[:]), r=[("H", 0), ("H", 1)], w=["dbg"], dma="dbg")

        P.barrier()
        OUTT = r3(R1[:, 0:16384].bitcast(F32), KC)

        def final_tile(tt):
            gcs = slice(tt * 512, (tt + 1) * 512)
            Hk = ("H", tt // 2)
            P.add("act", lambda e, gcs=gcs: e.activation(out=SQ[:], in_=hT[:, :, gcs], func=AF.Square),
                  r=[Hk], w=["SQ"])
            pb = ps_next()
            for kc in range(KC):
                P.add("pe", lambda e, kc=kc, pb=pb: e.matmul(
                    psb[pb], lhsT=tri[:, 2, :], rhs=SQ[:, kc, :], start=(kc == 0), stop=(kc == KC - 1)),
                    r=["SQ"] + CONST, w=[("ps", pb)])
            P.add("act", lambda e, pb=pb: e.activation(out=LNV, in_=psb[pb], func=AF.Ln, bias=EPS, scale=1.0 / D),
                  r=[("ps", pb)], w=["LNV"])
            P.add("act", lambda e: e.activation(out=RSTD, in_=LNV, func=AF.Exp, scale=-0.5), r=["LNV"], w=["RSTD"])
            for kc in range(KC):
                P.add("dve", lambda e, kc=kc, gcs=gcs: e.scalar_tensor_tensor(
                    out=OUTT[:, kc, :], in0=hT[:, kc, gcs], scalar=fgv[:, kc:kc + 1], in1=RSTD,
                    op0=ALU.mult, op1=ALU.mult), r=[Hk, "RSTD", "fgv"], w=[("OUTT", kc), "UT"])
            for b in range(4):
                s = (tt * 4 + b) % 2
                for half in range(2):
                    pb = ps_next()
                    for q in range(4):
                        kc = half * 4 + q
                        P.add("pe", lambda e, kc=kc, q=q, pb=pb, b=b: e.transpose(
                            psb[pb][:, q * 128:(q + 1) * 128], OUTT[:, kc, b * 128:(b + 1) * 128], identF[:]),
                            r=[("OUTT", kc), "UT"] + CONST, w=[("ps", pb)])
                    if half == 0:
                        P.add("act", lambda e, pb=pb, s=s: e.copy(out=XS[s][:, 0:512], in_=psb[pb]),
                              r=[("ps", pb)], w=[("xs", s)])
                    else:
                        P.add("dve", lambda e, pb=pb, s=s: e.tensor_copy(out=XS[s][:, 512:1024], in_=psb[pb]),
                              r=[("ps", pb)], w=[("xs", s)])
                row = (tt * 4 + b) * 128
                P.add("sp", lambda e, s=s, row=row: e.dma_start(out=y_d[row:row + 128, :], in_=XS[s]),
                      r=[("xs", s)], w=["ydram"], dma=("xs", s))

        def n3(st, tt):
            norm_mod(("H", st), hT[:, :, st * 1024:(st + 1) * 1024], 2, 2, "UT", UT, tiles=[tt])

        n3(0, 0)
        n3(0, 1)
        ffn(("H", 0), hT[:, :, 0:1024], 2, 1, None, {3: [lambda: n3(1, 0)], 5: [lambda: n3(1, 1)]})
        ffn(("H", 1), hT[:, :, 1024:2048], 2, 1, None, {1: [lambda: final_tile(0)], 4: [lambda: final_tile(1)]})
        final_tile(2)
        final_tile(3)

        P.emit(nc, block, sems)
    return nc, P


def _host_layouts(inp):
    f = np.float32
    g = {}
    w = np.asarray(inp["w_ada"][0], f)
    g["wada"] = np.ascontiguousarray(w.T).reshape(72, 128, 1024)
    for n, (kgu, kd) in enumerate((("ffn1_w_gu", "ffn1_w_down"), ("ffn2_w_gu", "ffn2_w_down"))):
        W = np.asarray(inp[kgu][0], f)
        Wg = W[:, :DFF].reshape(8, 128, 11, 2, 128)
        Wu = W[:, DFF:].reshape(8, 128, 11, 2, 128)
        S = np.stack([Wg, Wu], axis=4)
        g[f"wgu{n + 1}"] = np.ascontiguousarray(S.transpose(2, 1, 3, 0, 4, 5)).reshape(11, 128, 4096)
        Wd = np.asarray(inp[kd][0], f).reshape(NJ, 128, 8, 128)
        g[f"wd{n + 1}"] = np.ascontiguousarray(Wd.transpose(2, 1, 0, 3)).reshape(8, 128, DFF)
    W = np.asarray(inp["w_mix_in"][0], f)
    g["wmix"] = np.ascontiguousarray(W.reshape(8, 128, 10, 512).transpose(2, 1, 0, 3)).reshape(10, 128, 4096)
    for k, n in (("w_conv_out", "wco"), ("w_attn_out", "wao")):
        W = np.asarray(inp[k][0], f)
        g[n] = np.ascontiguousarray(W.reshape(4, 128, 8, 128).transpose(1, 2, 0, 3)).reshape(128, 4096)
    W = np.asarray(inp["w_out"][0], f)
    g["wo"] = np.ascontiguousarray(W.reshape(8, 128, 2, 4, 128).transpose(2, 1, 3, 0, 4)).reshape(2, 128, 4096)

    def v8(a):
        return np.asarray(a, f).reshape(8, 128).T

    g["vecs"] = np.ascontiguousarray(np.concatenate([
        v8(inp["norm1_g"][0]), v8(inp["norm2_g"][0]), v8(inp["norm3_g"][0]), v8(inp["final_g"]),
        v8(inp["b_merge"][0, 0]), v8(inp["b_merge"][0, 1])], axis=1))
    g["bada"] = np.ascontiguousarray(np.asarray(inp["b_ada"][0], f).reshape(72, 128).T)
    g["convw"] = np.ascontiguousarray(np.asarray(inp["conv_w"][0], f).reshape(3, 4, 128).transpose(2, 1, 0)).reshape(128, 12)
    g["ident"] = np.eye(128, dtype=f)
    jj = np.arange(128)[:, None]
    ss = np.arange(128)[None, :]
    tri = np.zeros((128, 3, 128), f)
    tri[:, 0, :] = -(jj >= ss).astype(f)
    tri[:, 1, :] = -(jj < ss).astype(f)
    tri[:, 2, :] = 1.0
    g["tri"] = tri
    return g


def _masks(r):
    f = np.float32
    s_ = np.arange(128)[:, None]
    t_ = np.arange(128)[None, :]
    trim = (s_ < t_).astype(f)
    mk = np.ascontiguousarray(np.stack([trim, trim], axis=1))
    om = np.full((128, 1), 1.0 if r == 1 else 0.0, f)
    return mk, om


_CACHE = {}


def kernel(**inputs):
    x = np.asarray(inputs["x"], np.float32)
    c = np.asarray(inputs["c"], np.float32)
    if "nc" not in _CACHE:
        _CACHE["nc"] = build_program()
    nc, P = _CACHE["nc"]
    g = _host_layouts(inputs)
    in_maps = []
    for core in range(8):
        b, r = core // 2, core % 2
        xb = x[b].reshape(32, 128, D)
        xo = xb[r::2].reshape(T, D)
        if r == 1:
            xp = xb[0::2].reshape(T, D)
        else:
            xp = np.concatenate([np.zeros((1, 128, D), np.float32), xb[1::2][:15]], 0).reshape(T, D)
        m = dict(g)
        m["xo"] = np.ascontiguousarray(xo)
        m["xp"] = np.ascontiguousarray(xp)
        m["crep"] = np.ascontiguousarray(np.broadcast_to(c[b][None, :], (128, D)))
        m["mk"], m["om"] = _masks(r)
        in_maps.append(m)
    res = run_bass_kernel_spmd(nc, in_maps, core_ids=list(range(8)))
    out = np.zeros((4, 32, 128, D), np.float32)
    for core in range(8):
        b, r = core // 2, core % 2
        out[b, r::2] = np.asarray(res.results[core]["y"], np.float32).reshape(16, 128, D)
    _CACHE["last"] = res
    return out.reshape(4, 4096, D)
```
